# Optimizing a Trainium2 kernel written in Bass

```python
import math
import jax, jax.numpy as jnp
from jax import lax
import numpy as np

D_MODEL = 1024
BATCH = 8
SEQ = 4096
DEPTH = 2
DEC_BATCH = 8
DEC_SEQ = 8192
PAST_LEN = 128

N_HEADS = 8
Q_LORA = 384
KV_LORA = 256
QK_NOPE = 64
QK_ROPE = 32
V_HEAD = 64
MLA_W = N_HEADS * V_HEAD
ROPE_THETA = 10000.0
Q_BLOCK = 128
GM_GROUPS = 4
GM_GROUP_CH = 128
GM_W = GM_GROUPS * GM_GROUP_CH
GM_CHUNK = 128
MEM_LEN = 256
MEM_HEADS = 4
MEM_HEAD_DIM = 128
MEM_W = MEM_HEADS * MEM_HEAD_DIM
N_BRANCH = 3
D_FF = 2816
CONV_W = 3
LN_EPS = 1e-5
RMS_EPS = 1e-6
ALPHA = (2 * DEPTH) ** 0.25
BETA = (8 * DEPTH) ** -0.25
IN_SPLITS = (Q_LORA, KV_LORA, QK_ROPE, GM_W, GM_W, MEM_W, N_BRANCH * D_MODEL)
IN_WIDTH = Q_LORA + KV_LORA + QK_ROPE + 2 * GM_W + MEM_W + N_BRANCH * D_MODEL

kernel_name = 'hybrid_mla_gmlp_mem_encoder'


def _split_offsets():
    offs = []
    t = 0
    for w in IN_SPLITS[:-1]:
        t += w
        offs.append(t)
    return offs


def layer_norm(x, g, b):
    xf = x.astype(jnp.float32)
    mu = jnp.mean(xf, axis=-1, keepdims=True)
    var = jnp.mean(jnp.square(xf - mu), axis=-1, keepdims=True)
    y = (xf - mu) * lax.rsqrt(var + LN_EPS) * g.astype(jnp.float32) + b.astype(jnp.float32)
    return y.astype(x.dtype)


def rms_norm(x, g):
    xf = x.astype(jnp.float32)
    y = xf * lax.rsqrt(jnp.mean(jnp.square(xf), axis=-1, keepdims=True) + RMS_EPS) * g.astype(jnp.float32)
    return y.astype(x.dtype)


def rope_tables(seq_len):
    pos = jnp.arange(seq_len, dtype=jnp.float32)
    inv = ROPE_THETA ** (-jnp.arange(0, QK_ROPE, 2, dtype=jnp.float32) / QK_ROPE)
    ang = pos[:, None] * inv[None, :]
    return jnp.cos(ang), jnp.sin(ang)


def apply_rope(x, cos, sin):
    x1, x2 = jnp.split(x, 2, axis=-1)
    return jnp.concatenate([x1 * cos - x2 * sin, x1 * sin + x2 * cos], axis=-1)


def mla_branch(c_q, c_kv, k_r, cos, sin, g_q, w_uq, g_kv, w_ukv):
    b, s, _ = c_q.shape
    q = (rms_norm(c_q, g_q) @ w_uq).reshape(b, s, N_HEADS, QK_NOPE + QK_ROPE)
    q_nope, q_rope = q[..., :QK_NOPE], q[..., QK_NOPE:]
    q_rope = apply_rope(q_rope, cos[:, None, :], sin[:, None, :])
    kv = (rms_norm(c_kv, g_kv) @ w_ukv).reshape(b, s, N_HEADS, QK_NOPE + V_HEAD)
    k_nope, v = kv[..., :QK_NOPE], kv[..., QK_NOPE:]
    k_rope = apply_rope(k_r, cos, sin)
    scale = (QK_NOPE + QK_ROPE) ** -0.5
    nb = s // Q_BLOCK
    qn = q_nope.reshape(b, nb, Q_BLOCK, N_HEADS, QK_NOPE).swapaxes(0, 1)
    qr = q_rope.reshape(b, nb, Q_BLOCK, N_HEADS, QK_ROPE).swapaxes(0, 1)

    def block(args):
        qn_b, qr_b = args
        sc = jnp.einsum('bqhd,bkhd->bhqk', qn_b, k_nope) + jnp.einsum('bqhr,bkr->bhqk', qr_b, k_rope)
        p = jax.nn.softmax(sc.astype(jnp.float32) * scale, axis=-1).astype(v.dtype)
        return jnp.einsum('bhqk,bkhd->bqhd', p, v)

    o = lax.map(block, (qn, qr))
    return o.swapaxes(0, 1).reshape(b, s, MLA_W)


def gmlp_branch(z_u, z_v, gm_ln_g, gm_ln_b, w_s, b_s):
    b, s, _ = z_u.shape
    u = jax.nn.gelu(z_u)
    v = layer_norm(jax.nn.gelu(z_v), gm_ln_g, gm_ln_b)
    nc = s // GM_CHUNK
    v = v.reshape(b, nc, GM_CHUNK, GM_GROUPS, GM_GROUP_CH)
    sp = jnp.einsum('gpq,bnqgc->bnpgc', w_s, v) + b_s.T[None, None, :, :, None]
    return u * sp.reshape(b, s, GM_W)


def mem_branch(q_in, mem, w_mem_kv):
    b, s, _ = q_in.shape
    m = mem.shape[1]
    q = q_in.reshape(b, s, MEM_HEADS, MEM_HEAD_DIM)
    kv = (mem @ w_mem_kv).reshape(b, m, 2, MEM_HEADS, MEM_HEAD_DIM)
    k, v = kv[:, :, 0], kv[:, :, 1]
    sc = jnp.einsum('bqhd,bkhd->bhqk', q, k).astype(jnp.float32) * (MEM_HEAD_DIM ** -0.5)
    p = jax.nn.softmax(sc, axis=-1).astype(v.dtype)
    return jnp.einsum('bhqk,bkhd->bqhd', p, v).reshape(b, s, MEM_W)


def conv_ffn(x, w_ffn_in, conv_w, conv_b, w_ffn_out):
    s = x.shape[1]
    h = x @ w_ffn_in
    pad = CONV_W // 2
    hp = jnp.pad(h, ((0, 0), (pad, pad), (0, 0)))
    hc = conv_b
    for t in range(CONV_W):
        hc = hc + hp[:, t:t + s] * conv_w[t]
    a, g = jnp.split(hc, 2, axis=-1)
    return (jax.nn.silu(a) * g) @ w_ffn_out


def trunk(x, mem, w_in, g_q, w_uq, g_kv, w_ukv, gm_ln_g, gm_ln_b, w_s, b_s, w_mem_kv,
          b_gate, w_br_mla, w_br_gmlp, w_br_mem, w_o, ln1_g, ln1_b,
          w_ffn_in, conv_w, conv_b, w_ffn_out, ln2_g, ln2_b):
    b, s, d = x.shape
    cos, sin = rope_tables(s)
    cos = cos.astype(x.dtype)
    sin = sin.astype(x.dtype)
    offs = _split_offsets()
    for l in range(DEPTH):
        z = x @ w_in[l]
        c_q, c_kv, k_r, z_u, z_v, q_mem, gate_pre = jnp.split(z, offs, axis=-1)
        o_a = mla_branch(c_q, c_kv, k_r, cos, sin, g_q[l], w_uq[l], g_kv[l], w_ukv[l])
        o_b = gmlp_branch(z_u, z_v, gm_ln_g[l], gm_ln_b[l], w_s[l], b_s[l])
        o_c = mem_branch(q_mem, mem, w_mem_kv[l])
        gates = jax.nn.sigmoid(gate_pre + b_gate[l]).reshape(b, s, N_BRANCH, d)
        merged = (gates[:, :, 0] * (o_a @ w_br_mla[l])
                  + gates[:, :, 1] * (o_b @ w_br_gmlp[l])
                  + gates[:, :, 2] * (o_c @ w_br_mem[l]))
        x = layer_norm(ALPHA * x + merged @ w_o[l], ln1_g[l], ln1_b[l])
        x = layer_norm(ALPHA * x + conv_ffn(x, w_ffn_in[l], conv_w[l], conv_b[l], w_ffn_out[l]),
                       ln2_g[l], ln2_b[l])
    return x


def setup_inputs(seed: int = 0) -> dict:
    key = jax.random.key(seed)
    ks = jax.random.split(key, 32)

    def nrm(k, shape, scale):
        return jax.random.normal(k, shape, jnp.float32) * scale

    def gain(k, shape):
        return 1.0 + 0.02 * jax.random.normal(k, shape, jnp.float32)

    return {
        'x_prompt': nrm(ks[0], (BATCH, SEQ, D_MODEL), 1.0),
        'x_sample': nrm(ks[1], (DEC_BATCH, DEC_SEQ, D_MODEL), 1.0),
        'mem_prompt': nrm(ks[2], (BATCH, MEM_LEN, D_MODEL), 1.0),
        'mem_sample': nrm(ks[3], (DEC_BATCH, MEM_LEN, D_MODEL), 1.0),
        'w_in': nrm(ks[4], (DEPTH, D_MODEL, IN_WIDTH), D_MODEL ** -0.5),
        'g_q': gain(ks[5], (DEPTH, Q_LORA)),
        'w_uq': nrm(ks[6], (DEPTH, Q_LORA, N_HEADS * (QK_NOPE + QK_ROPE)), Q_LORA ** -0.5),
        'g_kv': gain(ks[7], (DEPTH, KV_LORA)),
        'w_ukv': nrm(ks[8], (DEPTH, KV_LORA, N_HEADS * (QK_NOPE + V_HEAD)), KV_LORA ** -0.5),
        'gm_ln_g': gain(ks[9], (DEPTH, GM_W)),
        'gm_ln_b': nrm(ks[10], (DEPTH, GM_W), 0.02),
        'w_s': nrm(ks[11], (DEPTH, GM_GROUPS, GM_CHUNK, GM_CHUNK), GM_CHUNK ** -0.5),
        'b_s': gain(ks[12], (DEPTH, GM_GROUPS, GM_CHUNK)),
        'w_mem_kv': nrm(ks[13], (DEPTH, D_MODEL, 2 * MEM_W), D_MODEL ** -0.5),
        'b_gate': nrm(ks[14], (DEPTH, N_BRANCH * D_MODEL), 0.02),
        'w_br_mla': nrm(ks[15], (DEPTH, MLA_W, D_MODEL), BETA * MLA_W ** -0.5),
        'w_br_gmlp': nrm(ks[16], (DEPTH, GM_W, D_MODEL), BETA * GM_W ** -0.5),
        'w_br_mem': nrm(ks[17], (DEPTH, MEM_W, D_MODEL), BETA * MEM_W ** -0.5),
        'w_o': nrm(ks[18], (DEPTH, D_MODEL, D_MODEL), BETA * D_MODEL ** -0.5),
        'ln1_g': gain(ks[19], (DEPTH, D_MODEL)),
        'ln1_b': nrm(ks[20], (DEPTH, D_MODEL), 0.02),
        'w_ffn_in': nrm(ks[21], (DEPTH, D_MODEL, 2 * D_FF), D_MODEL ** -0.5),
        'conv_w': nrm(ks[22], (DEPTH, CONV_W, 2 * D_FF), CONV_W ** -0.5),
        'conv_b': nrm(ks[23], (DEPTH, 2 * D_FF), 0.02),
        'w_ffn_out': nrm(ks[24], (DEPTH, D_FF, D_MODEL), BETA * D_FF ** -0.5),
        'ln2_g': gain(ks[25], (DEPTH, D_MODEL)),
        'ln2_b': nrm(ks[26], (DEPTH, D_MODEL), 0.02),
    }


def reference(x_prompt, x_sample, mem_prompt, mem_sample, w_in, g_q, w_uq, g_kv, w_ukv,
              gm_ln_g, gm_ln_b, w_s, b_s, w_mem_kv, b_gate, w_br_mla, w_br_gmlp, w_br_mem,
              w_o, ln1_g, ln1_b, w_ffn_in, conv_w, conv_b, w_ffn_out, ln2_g, ln2_b):
    y_prompt = trunk(x_prompt, mem_prompt, w_in, g_q, w_uq, g_kv, w_ukv, gm_ln_g, gm_ln_b, w_s, b_s,
                     w_mem_kv, b_gate, w_br_mla, w_br_gmlp, w_br_mem, w_o, ln1_g, ln1_b,
                     w_ffn_in, conv_w, conv_b, w_ffn_out, ln2_g, ln2_b)
    y_sample = trunk(x_sample, mem_sample, w_in, g_q, w_uq, g_kv, w_ukv, gm_ln_g, gm_ln_b, w_s, b_s,
                     w_mem_kv, b_gate, w_br_mla, w_br_gmlp, w_br_mem, w_o, ln1_g, ln1_b,
                     w_ffn_in, conv_w, conv_b, w_ffn_out, ln2_g, ln2_b)
    return (y_prompt, y_sample)
```

```python
import math
from contextlib import ExitStack

import numpy as np
import concourse.bass as bass
import concourse.mybir as mybir
from concourse.bass_utils import run_bass_kernel_spmd

F32 = mybir.dt.float32
BF16 = mybir.dt.bfloat16
AF = mybir.ActivationFunctionType
ALU = mybir.AluOpType

D = 1024
KC = 8
T = 512
L = 2
NH = 8
QL = 384
KVL = 256
DFF = 2816
NPAIR = 22
INW = 5280
ALPHA = (2 * L) ** 0.25
LN_EPS = 1e-5
RMS_EPS = 1e-6
SM_SCALE = 96 ** -0.5
MEM_SCALE = 128 ** -0.5

WNAMES = ['w_in', 'g_q', 'w_uq', 'g_kv', 'w_ukv', 'gm_ln_g', 'gm_ln_b', 'w_s', 'b_s', 'w_mem_kv',
          'b_gate', 'w_br_mla', 'w_br_gmlp', 'w_br_mem', 'w_o', 'ln1_g', 'ln1_b',
          'w_ffn_in', 'conv_w', 'conv_b', 'w_ffn_out', 'ln2_g', 'ln2_b']
WSHAPES = {
    'w_in': [L, D, INW], 'g_q': [L, QL], 'w_uq': [L, QL, 768], 'g_kv': [L, KVL], 'w_ukv': [L, KVL, 1024],
    'gm_ln_g': [L, 512], 'gm_ln_b': [L, 512], 'w_s': [L, 4, 128, 128], 'b_s': [L, 4, 128],
    'w_mem_kv': [L, D, 1024], 'b_gate': [L, 3072], 'w_br_mla': [L, 512, D], 'w_br_gmlp': [L, 512, D],
    'w_br_mem': [L, 512, D], 'w_o': [L, D, D], 'ln1_g': [L, D], 'ln1_b': [L, D],
    'w_ffn_in': [L, D, 2 * DFF], 'conv_w': [L, 3, 2 * DFF], 'conv_b': [L, 2 * DFF],
    'w_ffn_out': [L, DFF, D], 'ln2_g': [L, D], 'ln2_b': [L, D],
}
BFW = ['w_in', 'w_uq', 'w_ukv', 'w_s', 'w_mem_kv', 'w_br_mla', 'w_br_gmlp', 'w_br_mem', 'w_o',
       'w_ffn_in', 'w_ffn_out']


class _FirstWait:
    def __init__(self, eng, tok):
        self._eng = eng
        self._tok = tok
        self._done = False

    def __getattr__(self, name):
        attr = getattr(self._eng, name)
        if self._done or not callable(attr):
            return attr

        def wrapped(*a, **k):
            r = attr(*a, **k)
            if not self._done and hasattr(r, '_wait_ge'):
                r._wait_ge(self._tok[1], self._tok[2])
                self._done = True
            return r
        return wrapped


class Prog:
    def __init__(self, nc):
        self.nc = nc
        self.eng = {'pe': nc.tensor, 'act': nc.scalar, 'dve': nc.vector, 'pool': nc.gpsimd, 'sp': nc.sync}
        self.esem = {e: nc.alloc_semaphore("sem_" + e) for e in ('pe', 'act', 'dve', 'pool')}
        self.ecnt = {e: 0 for e in self.esem}
        self.dsems = {}
        self.last_w = {}
        self.readers = {}
        self.waited = {e: {} for e in self.eng}
        self.rr = 0

    def _wait(self, e, tok):
        name, sem, val = tok
        if self.waited[e].get(name, 0) >= val:
            return
        self.waited[e][name] = val
        self.eng[e].wait_ge(sem, val)

    def op(self, e, fn, r=(), w=(), dsem=None, after=()):
        toks = [t for t in after if t is not None]
        for x in r:
            t = self.last_w.get(x)
            if t is not None:
                toks.append(t)
        for x in w:
            t = self.last_w.get(x)
            if t is not None:
                toks.append(t)
            toks.extend(self.readers.get(x, {}).values())
        need = []
        seen = {}
        for t in toks:
            if e == 'pe' and t[0] == 'pe':
                continue
            if self.waited[e].get(t[0], 0) >= t[2]:
                continue
            if t[0] not in seen or seen[t[0]][2] < t[2]:
                seen[t[0]] = t
        need = list(seen.values())
        for t in need[:-1]:
            self._wait(e, t)
        if need:
            last = need[-1]
            self.waited[e][last[0]] = last[2]
            ins = fn(_FirstWait(self.eng[e], last))
        else:
            ins = fn(self.eng[e])
        if dsem is not None:
            if not isinstance(ins, (list, tuple)):
                ins = [ins]
            ent = self.dsems.get(dsem)
            if ent is None:
                ent = [self.nc.alloc_semaphore("d_" + dsem), 0]
                self.dsems[dsem] = ent
            for i_ in ins:
                ent[1] += 16
                i_.then_inc(ent[0], 16)
            tok = ("d_" + dsem, ent[0], ent[1])
        else:
            self.ecnt[e] += 1
            ins.then_inc(self.esem[e], 1)
            tok = (e, self.esem[e], self.ecnt[e])
        for x in w:
            self.last_w[x] = tok
            self.readers[x] = {}
        for x in r:
            d = self.readers.setdefault(x, {})
            o = d.get(tok[0])
            if o is None or o[2] < tok[2]:
                d[tok[0]] = tok
        return tok

    def barrier(self, final=False):
        toks = [(e, self.esem[e], self.ecnt[e]) for e in self.esem if self.ecnt[e] > 0]
        toks += [("d_" + k, v[0], v[1]) for k, v in self.dsems.items() if v[1] > 0 and (final or not k.startswith('wc_'))]
        for e in self.eng:
            for t in toks:
                self._wait(e, t)
        self.last_w = {}
        self.readers = {}

    def psum(self):
        b = self.rr
        self.rr = (self.rr + 1) % 8
        return b


def build_nc(SP, SS, debug=False):
    nc = bass.Bass("TRN2", target_bir_lowering=False)
    SL = [SP, SS]
    SMAX = max(SL)
    NBMAX = SMAX // 128

    def din(name, shape, dt=F32):
        return nc.dram_tensor(name, shape, dt, kind="ExternalInput").ap()

    def dint(name, shape, dt):
        return nc.dram_tensor(name, shape, dt, kind="ExternalOutput" if debug else "Internal").ap()

    x_in = [din("xp", [SP, D]), din("xs", [SS, D])]
    mem_in = [din("memp", [256, D]), din("mems", [256, D])]
    W = {n: din(n, WSHAPES[n]) for n in WNAMES}
    identf = din("identf", [128, 128])
    ropeC = din("ropeC", [128, 8192])
    ropeS = din("ropeS", [128, 8192])
    y_out = [nc.dram_tensor("yp", [SP, D], F32, kind="ExternalOutput").ap(),
             nc.dram_tensor("ys", [SS, D], F32, kind="ExternalOutput").ap()]

    WB = {n: nc.dram_tensor(n + "_bf", WSHAPES[n], BF16, kind="Internal").ap() for n in BFW}
    sc = []
    for s in range(2):
        S = SL[s]
        sc.append(dict(
            xT=dint(f"xT{s}", [8, 128, S], BF16),
            qTn=dint(f"qTn{s}", [4, 128, S], BF16), qTr1=dint(f"qTr1{s}", [128, S], BF16),
            qTr2=dint(f"qTr2{s}", [128, S], BF16),
            kTn=dint(f"kTn{s}", [4, 128, S], BF16), kTr1=dint(f"kTr1{s}", [16, S], BF16),
            kTr2=dint(f"kTr2{s}", [16, S], BF16),
            oaU=dint(f"oaU{s}", [512, S], F32), sums=dint(f"sums{s}", [8, S], F32),
            brT=dint(f"brT{s}", [12, 128, S], BF16),
            y=dint(f"y{s}", [S, D], F32), yT=dint(f"yT{s}", [8, 128, S], BF16),
            part=dint(f"part{s}", [S + 1, D], F32),
            x1=dint(f"x1{s}", [S, D], F32),
        ))

    P = Prog(nc)
    op = P.op
    _uid = [0]

    def SBT(name, shape, dt):
        _uid[0] += 1
        return nc.sbuf_tensor(f"{name}_u{_uid[0]}", shape, dt)

    with ExitStack() as gs:
        psall = gs.enter_context(nc.psum_tensor("psall", [128, 4096], F32))
        identF = gs.enter_context(nc.sbuf_tensor("identF", [128, 128], F32))
        identB = gs.enter_context(nc.sbuf_tensor("identB", [128, 128], BF16))
        onesB = gs.enter_context(nc.sbuf_tensor("onesB", [128, 128], BF16))
        VresBox = [None]

        def PS(b):
            return psall[:, b * 512:(b + 1) * 512]

        def PSB16(b):
            return PS(b).bitcast(BF16)

        def PSW(g):
            return psall[:, g * 1024:(g + 1) * 1024]

        op('sp', lambda e: e.dma_start(out=identF[:], in_=identf[:, :]), w=['identF'], dsem='c0')
        op('dve', lambda e: e.tensor_copy(out=identB[:], in_=identF[:]), r=['identF'], w=['identB'])
        op('dve', lambda e: e.memset(onesB[:], 1.0), w=['onesB'])
        P.barrier()
        wtok = {}
        w_in_src = W['w_in'].rearrange("l r n -> (l r) n")
        w_in_dst = WB['w_in'].rearrange("l r n -> (l r) n")
        W_IN_GROUPS = [(0, 672), (672, 2208), (2208, INW)]

        cast_queue = []

        def cast_w_in(gi_, queue=False):
            c0_, c1_ = W_IN_GROUPS[gi_]
            for i in range(2 * L):
                def piece(after=None, i=i):
                    wtok['w_in%d' % gi_] = op('pool', lambda e: e.dma_start(out=w_in_dst[i * 512:(i + 1) * 512, c0_:c1_],
                                                                           in_=w_in_src[i * 512:(i + 1) * 512, c0_:c1_]),
                                              w=[('wb', 'w_in', gi_, i)], dsem='wc_w_in%d' % gi_, after=[after])
                if queue:
                    cast_queue.append(piece)
                else:
                    piece()

        def cast_w(n, queue=False):
            src, dst = W[n], WB[n]
            if n == 'w_s':
                src = src.rearrange("l g p q -> (l g p) q")
                dst = dst.rearrange("l g p q -> (l g p) q")
            else:
                src = src.rearrange("l r n -> (l r) n")
                dst = dst.rearrange("l r n -> (l r) n")
            R = src.shape[0]
            nblk = max(1, R // 512)
            rb = R // nblk
            for i in range(nblk):
                def piece(after=None, i=i):
                    wtok[n] = op('pool', lambda e: e.dma_start(out=dst[i * rb:(i + 1) * rb, :], in_=src[i * rb:(i + 1) * rb, :]),
                                 w=[('wb', n, i)], dsem='wc_' + n, after=[after])
                if queue:
                    cast_queue.append(piece)
                else:
                    piece()

        cast_w_in(0)
        cast_w('w_uq')
        cast_w('w_ukv')
        for n in BFW:
            if n not in ('w_in', 'w_uq', 'w_ukv'):
                wtok[n] = None
        wtok['w_in1'] = wtok['w_in2'] = None
        cast_w_in(1, queue=True)
        for n in ('w_s', 'w_mem_kv'):
            cast_w(n, queue=True)
        cast_w_in(2, queue=True)
        for n in BFW:
            if n not in ('w_in', 'w_uq', 'w_ukv', 'w_s', 'w_mem_kv'):
                cast_w(n, queue=True)

        def cast_rest():
            while cast_queue:
                cast_queue.pop(0)()

        def rstd_ops(out_ap, in_ap, tmp_ap, scale_in, eps, lnmul, rkeys, wkey, tmpkey):
            op('act', lambda e: e.activation(out=tmp_ap, in_=in_ap, func=AF.Ln, bias=eps, scale=scale_in),
               r=rkeys, w=[tmpkey])
            op('act', lambda e: e.activation(out=out_ap, in_=tmp_ap, func=AF.Exp, bias=lnmul, scale=-0.5),
               r=[tmpkey], w=[wkey])

        def mm_group(ps_ap, pairs, r, w, start=True, stop=True):
            def fn(e):
                ins = None
                n = len(pairs)
                for i, (lh, rh) in enumerate(pairs):
                    ins = e.matmul(ps_ap, lh, rh, start=(start and i == 0), stop=(stop and i == n - 1))
                return ins
            return op('pe', fn, r=r, w=w)

        pending_cols = []

        def load_cols(st, name, rows_ap, nrows, tag):
            raw = st.enter_context(SBT(name + "_raw", [nrows, 128], F32))
            out = st.enter_context(SBT(name, [128, nrows], F32))
            op('sp', lambda e: e.dma_start(out=raw[:], in_=rows_ap), w=[name + '_raw'], dsem='c0')
            pending_cols.append((name, raw, out, nrows))
            return out

        def finish_cols():
            P.barrier()
            for (name, raw, out, nrows) in pending_cols:
                b = P.psum()
                mm_group(PS(b)[:, 0:nrows], [(raw[:], identF[0:nrows, 0:nrows])], r=[], w=[('ps', b)])
                op('dve', lambda e, out=out, b=b, nrows=nrows: e.tensor_copy(out=out[:], in_=PS(b)[:, 0:nrows]), r=[('ps', b)], w=[name])
            del pending_cols[:]
            P.barrier()

        def load_bcast(st, name, vec_ap, n):
            t = st.enter_context(SBT(name, [128, n], F32))
            op('sp', lambda e: e.dma_start(out=t[:], in_=vec_ap.partition_broadcast(128)), w=[name], dsem='c0')
            return t

        def ln_rows(buf, nsub, n, g_bc, b_bc, mv, tmp4, rs4, key, eps, prow=None, var_scale=1.0, out_scale=1.0,
                    stats=None, pre_r=(), gb_eng='pool', fin_out=None, fin_key=None, nmr=None, dve_sub=()):
            rows = slice(0, 128) if prow is None else prow
            nch = n // 512
            for j in range(nsub):
                for c in range(nch):
                    op('dve', lambda e, j=j, c=c: e.bn_stats(out=stats[rows, j, c, :], in_=buf[rows, j, c * 512:(c + 1) * 512]),
                       r=[(key, j)] + list(pre_r), w=[(key, 'st', j, c)])
                op('dve', lambda e, j=j: e.bn_aggr(out=mv[rows, j, :], in_=stats[rows, j, :, :]),
                   r=[(key, 'st', j, c) for c in range(nch)], w=[(key, 'mv', j)])
            rstd_ops(rs4[rows, 0:nsub], mv[rows, 0:nsub, 1], tmp4[rows, 0:nsub], var_scale, eps,
                     math.log(out_scale) if out_scale != 1.0 else 0.0,
                     [(key, 'mv', j) for j in range(nsub)], (key, 'rs'), (key, 'tmp4'))
            op('dve', lambda e: e.scalar_tensor_tensor(out=nmr[rows, 0:nsub], in0=mv[rows, 0:nsub, 0], scalar=-1.0, in1=rs4[rows, 0:nsub],
                                                       op0=ALU.mult, op1=ALU.mult),
               r=[(key, 'rs')] + [(key, 'mv', j) for j in range(nsub)], w=[(key, 'nmr')])
            for j in range(nsub):
                if j in dve_sub:
                    op('dve', lambda e, j=j: e.scalar_tensor_tensor(out=buf[rows, j, :], in0=buf[rows, j, :], scalar=mv[rows, j, 0:1],
                                                                    in1=g_bc[rows, :], op0=ALU.subtract, op1=ALU.mult),
                       r=[(key, j), (key, 'mv', j)], w=[(key, j)])
                    op('dve', lambda e, j=j: e.scalar_tensor_tensor(out=buf[rows, j, :], in0=buf[rows, j, :], scalar=rs4[rows, j:j + 1],
                                                                    in1=b_bc[rows, :], op0=ALU.mult, op1=ALU.add),
                       r=[(key, j), (key, 'rs')], w=[(key, j)])
                    continue
                op('act', lambda e, j=j: e.activation(out=buf[rows, j, :], in_=buf[rows, j, :], func=AF.Identity,
                                                      bias=nmr[rows, j:j + 1], scale=rs4[rows, j:j + 1]),
                   r=[(key, j), (key, 'nmr'), (key, 'rs')], w=[(key, j)])
                op(gb_eng, lambda e, j=j: e.tensor_tensor(out=buf[rows, j, :], in0=buf[rows, j, :], in1=g_bc[rows, :], op=ALU.mult),
                   r=[(key, j)], w=[(key, j)])
                if fin_out is None:
                    op(gb_eng, lambda e, j=j: e.tensor_tensor(out=buf[rows, j, :], in0=buf[rows, j, :], in1=b_bc[rows, :], op=ALU.add),
                       r=[(key, j)], w=[(key, j)])
                else:
                    op(gb_eng, lambda e, j=j: e.tensor_tensor(out=fin_out(j), in0=buf[rows, j, :], in1=b_bc[rows, :], op=ALU.add),
                       r=[(key, j)], w=[(fin_key, j)])

        def phaseA(l, s, x_src):
            S = SL[s]
            NT = S // T
            d = sc[s]
            with ExitStack() as st:
                def sb(name, shape, dt):
                    return st.enter_context(SBT(name, shape, dt))
                wA = sb("wA", [128, KC, 672], BF16)
                wqn = sb("wqn", [128, 3, 512], BF16)
                wqr1 = sb("wqr1", [128, 3, 128], BF16)
                wqr2 = sb("wqr2", [128, 3, 128], BF16)
                wkn = sb("wkn", [128, 2, 512], BF16)
                wv = sb("wv", [128, 2, 512], BF16)
                with nc.allow_non_contiguous_dma(reason="one-time weight column gathers"):
                    op('sp', lambda e: e.dma_start(out=wA[:], in_=WB['w_in'][l, :, 0:672].rearrange("(c p) n -> p c n", p=128)),
                       w=['wA'], dsem='wA', after=[wtok['w_in0']])
                    uq = WB['w_uq'][l].rearrange("(c p) (h d) -> p c h d", p=128, d=96)
                    op('sp', lambda e: [e.dma_start(out=wqn[:, c, :].rearrange("p (h d) -> p h d", d=64), in_=uq[:, c, :, 0:64]) for c in range(3)]
                       + [e.dma_start(out=wqr1[:, c, :].rearrange("p (h d) -> p h d", d=16), in_=uq[:, c, :, 64:80]) for c in range(3)]
                       + [e.dma_start(out=wqr2[:, c, :].rearrange("p (h d) -> p h d", d=16), in_=uq[:, c, :, 80:96]) for c in range(3)],
                       w=['wq'], dsem='wA', after=[wtok['w_uq']])
                    ukv = WB['w_ukv'][l].rearrange("(c p) (h d) -> p c h d", p=128, d=128)
                    op('sp', lambda e: [e.dma_start(out=wkn[:, c, :].rearrange("p (h d) -> p h d", d=64), in_=ukv[:, c, :, 0:64]) for c in range(2)]
                       + [e.dma_start(out=wv[:, c, :].rearrange("p (h d) -> p h d", d=64), in_=ukv[:, c, :, 64:128]) for c in range(2)],
                       w=['wkv'], dsem='wA', after=[wtok['w_ukv']])
                gq = load_cols(st, "gqc", W['g_q'][l].rearrange("(c p) -> c p", p=128), 3, 'gq')
                gkv = load_cols(st, "gkvc", W['g_kv'][l].rearrange("(c p) -> c p", p=128), 2, 'gkv')
                finish_cols()

                xt = [sb(f"xt{i}", [128, 4, D], F32) for i in range(2)]
                rC = [sb(f"rC{i}", [128, T], F32) for i in range(2)]
                rS = [sb(f"rS{i}", [128, T], F32) for i in range(2)]
                xb = sb("xb", [128, 4, D], BF16)
                xT = [sb(f"xTa{i}", [128, KC, T], BF16) for i in range(2)]
                cqg = sb("cqg", [128, 3, T], BF16)
                sq = sb("sq", [128, 3, T], BF16)
                ckvg = sb("ckvg", [128, 2, T], BF16)
                sqkv = sb("sqkv", [128, 2, T], BF16)
                rq = sb("rq", [128, T], F32)
                rkv = sb("rkv", [128, T], F32)
                lnt = sb("lnt", [128, T], F32)
                lnt4 = sb("lnt4", [128, 4], F32)
                rtok = sb("rtok", [128, 4], F32)
                x1f = sb("x1f", [128, T], F32)
                x2f = sb("x2f", [128, T], F32)
                Cr = sb("Cr", [128, T], F32)
                Sr = sb("Sr", [128, T], F32)
                ta = sb("ta", [128, T], F32)
                tb = sb("tb", [128, T], F32)
                qn_st = [sb(f"qn_st{i}", [128, 4, T], BF16) for i in range(2)]
                qr1_st = [sb(f"qr1_st{i}", [128, T], BF16) for i in range(2)]
                qr2_st = [sb(f"qr2_st{i}", [128, T], BF16) for i in range(2)]
                kn_st = [sb(f"kn_st{i}", [128, 4, T], BF16) for i in range(2)]
                kr1_st = [sb(f"kr1_st{i}", [16, T], BF16) for i in range(2)]
                kr2_st = [sb(f"kr2_st{i}", [16, T], BF16) for i in range(2)]

                def load(i):
                    sl = i % 2
                    t0 = i * T
                    op('sp', lambda e: e.dma_start(out=xt[sl][:], in_=x_src[t0:t0 + T, :].rearrange("(j p) d -> p j d", p=128)),
                       w=[('xt', sl)], dsem=f'xt{sl}')
                    op('sp', lambda e: [e.dma_start(out=rC[sl][:], in_=ropeC[:, t0:t0 + T]),
                                        e.dma_start(out=rS[sl][:], in_=ropeS[:, t0:t0 + T])],
                       w=[('rope', sl)], dsem=f'rope{sl}')

                def cast(i_):
                    for j in range(4):
                        if j % 2 == 0:
                            op('pool', lambda e, j=j: e.tensor_copy(out=xb[:, j, :], in_=xt[i_ % 2][:, j, :]),
                               r=[('xt', i_ % 2)], w=[('xb', j)])
                        else:
                            op('act', lambda e, j=j: e.activation(out=xb[:, j, :], in_=xt[i_ % 2][:, j, :], func=AF.Copy),
                               r=[('xt', i_ % 2)], w=[('xb', j)])

                def prep(i_):
                    sl_ = i_ % 2
                    t0_ = i_ * T
                    for m in range(4):
                        b = P.psum()

                        def tr(e):
                            ins = None
                            for cc in range(2):
                                c = 2 * m + cc
                                for j in range(4):
                                    ins = e.transpose(PSB16(b)[:, cc * 512 + j * 128: cc * 512 + (j + 1) * 128],
                                                      xb[:, j, c * 128:(c + 1) * 128], identB[:])
                            return ins
                        op('pe', tr, r=[('xb', j) for j in range(4)] + ['identB'], w=[('ps', b)])
                        if m % 2 == 0:
                            op('act', lambda e: e.activation(out=xT[sl_][:, 2 * m:2 * m + 2, :].rearrange("p a t -> p (a t)"), in_=PSB16(b), func=AF.Copy),
                               r=[('ps', b)], w=[('xT', sl_, m)])
                        else:
                            op('dve', lambda e: e.tensor_copy(out=xT[sl_][:, 2 * m:2 * m + 2, :].rearrange("p a t -> p (a t)"), in_=PSB16(b)),
                               r=[('ps', b)], w=[('xT', sl_, m)])
                    op('sp', lambda e: e.dma_start(out=d['xT'][:, :, t0_:t0_ + T].rearrange("c p t -> p c t"), in_=xT[sl_][:]),
                       r=[('xT', sl_, m) for m in range(4)], dsem=f'xTst{sl_}')

                load(0)
                cast(0)
                prep(0)
                for i in range(NT):
                    sl = i % 2
                    t0 = i * T
                    if i + 1 < NT:
                        load(i + 1)
                    xTk = [('xT', sl, m) for m in range(4)]
                    for m in range(3):
                        b = P.psum()
                        mm_group(PS(b), [(wA[:, k, m * 128:(m + 1) * 128], xT[sl][:, k, :]) for k in range(KC)],
                                 r=xTk + ['wA'], w=[('ps', b)])
                        op('act', lambda e, m=m, b=b: e.activation(out=cqg[:, m, :], in_=PS(b), func=AF.Identity, scale=gq[:, m:m + 1]),
                           r=[('ps', b), 'gqc'], w=[('cqg', m)])
                        op('act', lambda e, m=m, b=b: e.activation(out=sq[:, m, :], in_=PS(b), func=AF.Square),
                           r=[('ps', b)], w=[('sq', m)])
                    b = P.psum()
                    mm_group(PS(b), [(onesB[:], sq[:, m, :]) for m in range(3)], r=[('sq', m) for m in range(3)] + ['onesB'],
                             w=[('ps', b)])
                    rstd_ops(rq[:], PS(b), lnt[:], 1.0 / QL, RMS_EPS, math.log(SM_SCALE), [('ps', b)], 'rq', 'lnt')
                    for j in range(4):
                        b = P.psum()
                        mm_group(PS(b), [(wqn[:, k, j * 128:(j + 1) * 128], cqg[:, k, :]) for k in range(3)],
                                 r=[('cqg', m) for m in range(3)] + ['wq'], w=[('ps', b)])
                        op('dve', lambda e, j=j, b=b: e.tensor_tensor(out=qn_st[sl][:, j, :], in0=PS(b), in1=rq[:], op=ALU.mult),
                           r=[('ps', b), 'rq'], w=[('qn_st', sl)])
                    if i + 1 < NT:
                        cast(i + 1)
                    b1 = P.psum()
                    mm_group(PS(b1), [(wqr1[:, k, :], cqg[:, k, :]) for k in range(3)], r=[('cqg', m) for m in range(3)] + ['wq'],
                             w=[('ps', b1)])
                    b2 = P.psum()
                    mm_group(PS(b2), [(wqr2[:, k, :], cqg[:, k, :]) for k in range(3)], r=[('cqg', m) for m in range(3)] + ['wq'],
                             w=[('ps', b2)])
                    op('act', lambda e: e.activation(out=x1f[:], in_=PS(b1), func=AF.Copy), r=[('ps', b1)], w=['x1f'])
                    op('act', lambda e: e.activation(out=x2f[:], in_=PS(b2), func=AF.Copy), r=[('ps', b2)], w=['x2f'])
                    op('pool', lambda e: e.tensor_tensor(out=Cr[:], in0=rC[sl][:], in1=rq[:], op=ALU.mult), r=[('rope', sl), 'rq'], w=['Cr'])
                    op('pool', lambda e: e.tensor_tensor(out=Sr[:], in0=rS[sl][:], in1=rq[:], op=ALU.mult), r=[('rope', sl), 'rq'], w=['Sr'])
                    op('dve', lambda e: e.tensor_tensor(out=ta[:], in0=x1f[:], in1=Cr[:], op=ALU.mult), r=['x1f', 'Cr'], w=['ta'])
                    op('pool', lambda e: e.tensor_tensor(out=tb[:], in0=x2f[:], in1=Sr[:], op=ALU.mult), r=['x2f', 'Sr'], w=['tb'])
                    op('dve', lambda e: e.tensor_tensor(out=qr1_st[sl][:], in0=ta[:], in1=tb[:], op=ALU.subtract), r=['ta', 'tb'], w=[('qr1_st', sl)])
                    op('dve', lambda e: e.tensor_tensor(out=ta[:], in0=x1f[:], in1=Sr[:], op=ALU.mult), r=['x1f', 'Sr'], w=['ta'])
                    op('pool', lambda e: e.tensor_tensor(out=tb[:], in0=x2f[:], in1=Cr[:], op=ALU.mult), r=['x2f', 'Cr'], w=['tb'])
                    op('dve', lambda e: e.tensor_tensor(out=qr2_st[sl][:], in0=ta[:], in1=tb[:], op=ALU.add), r=['ta', 'tb'], w=[('qr2_st', sl)])
                    op('sp', lambda e: [e.dma_start(out=d['qTn'][:, :, t0:t0 + T].rearrange("j p t -> p j t"), in_=qn_st[sl][:]),
                                        e.dma_start(out=d['qTr1'][:, t0:t0 + T], in_=qr1_st[sl][:]),
                                        e.dma_start(out=d['qTr2'][:, t0:t0 + T], in_=qr2_st[sl][:])],
                       r=[('qn_st', sl), ('qr1_st', sl), ('qr2_st', sl)], dsem=f'qst{sl}')
                    for m in range(2):
                        b = P.psum()
                        mm_group(PS(b), [(wA[:, k, QL + m * 128:QL + (m + 1) * 128], xT[sl][:, k, :]) for k in range(KC)],
                                 r=xTk + ['wA'], w=[('ps', b)])
                        op('act', lambda e, m=m, b=b: e.activation(out=ckvg[:, m, :], in_=PS(b), func=AF.Identity, scale=gkv[:, m:m + 1]),
                           r=[('ps', b), 'gkvc'], w=[('ckvg', m)])
                        op('act', lambda e, m=m, b=b: e.activation(out=sqkv[:, m, :], in_=PS(b), func=AF.Square),
                           r=[('ps', b)], w=[('sqkv', m)])
                    b = P.psum()
                    mm_group(PS(b), [(onesB[:], sqkv[:, m, :]) for m in range(2)], r=[('sqkv', m) for m in range(2)] + ['onesB'],
                             w=[('ps', b)])
                    rstd_ops(rkv[:], PS(b), lnt[:], 1.0 / KVL, RMS_EPS, 0.0, [('ps', b)], 'rkv', 'lnt')
                    b = P.psum()

                    def ssq_tok(e, b=b):
                        ins = None
                        for j in range(4):
                            for m in range(2):
                                ins = e.matmul(PS(b)[:, j:j + 1], sqkv[:, m, j * 128:(j + 1) * 128], onesB[:, 0:1],
                                               start=(m == 0), stop=(m == 1))
                        return ins
                    op('pe', ssq_tok, r=[('sqkv', m) for m in range(2)] + ['onesB'], w=[('ps', b)])
                    rstd_ops(rtok[:], PS(b)[:, 0:4], lnt4[:], 1.0 / KVL, RMS_EPS, 0.0, [('ps', b)], 'rtok', 'lnt4')
                    for j in range(4):
                        b = P.psum()
                        mm_group(PS(b), [(wkn[:, k, j * 128:(j + 1) * 128], ckvg[:, k, :]) for k in range(2)],
                                 r=[('ckvg', m) for m in range(2)] + ['wkv'], w=[('ps', b)])
                        op('dve', lambda e, j=j, b=b: e.tensor_tensor(out=kn_st[sl][:, j, :], in0=PS(b), in1=rkv[:], op=ALU.mult),
                           r=[('ps', b), 'rkv'], w=[('kn_st', sl)])
                    b1 = P.psum()
                    mm_group(PS(b1)[0:16, :], [(wA[:, k, 640:656], xT[sl][:, k, :]) for k in range(KC)], r=xTk + ['wA'], w=[('ps', b1)])
                    b2 = P.psum()
                    mm_group(PS(b2)[0:16, :], [(wA[:, k, 656:672], xT[sl][:, k, :]) for k in range(KC)], r=xTk + ['wA'], w=[('ps', b2)])
                    op('act', lambda e: e.activation(out=x1f[0:16, :], in_=PS(b1)[0:16, :], func=AF.Copy), r=[('ps', b1), ('qr1_st', sl), ('qr2_st', sl)], w=['x1f'])
                    op('act', lambda e: e.activation(out=x2f[0:16, :], in_=PS(b2)[0:16, :], func=AF.Copy), r=[('ps', b2), ('qr1_st', sl), ('qr2_st', sl)], w=['x2f'])
                    op('dve', lambda e: e.tensor_tensor(out=ta[0:16, :], in0=x1f[0:16, :], in1=rC[sl][0:16, :], op=ALU.mult), r=['x1f', ('rope', sl)], w=['ta'])
                    op('pool', lambda e: e.tensor_tensor(out=tb[0:16, :], in0=x2f[0:16, :], in1=rS[sl][0:16, :], op=ALU.mult), r=['x2f', ('rope', sl)], w=['tb'])
                    op('dve', lambda e: e.tensor_tensor(out=kr1_st[sl][:], in0=ta[0:16, :], in1=tb[0:16, :], op=ALU.subtract), r=['ta', 'tb'], w=[('kr1_st', sl)])
                    op('dve', lambda e: e.tensor_tensor(out=ta[0:16, :], in0=x1f[0:16, :], in1=rS[sl][0:16, :], op=ALU.mult), r=['x1f', ('rope', sl)], w=['ta'])
                    op('pool', lambda e: e.tensor_tensor(out=tb[0:16, :], in0=x2f[0:16, :], in1=rC[sl][0:16, :], op=ALU.mult), r=['x2f', ('rope', sl)], w=['tb'])
                    op('dve', lambda e: e.tensor_tensor(out=kr2_st[sl][:], in0=ta[0:16, :], in1=tb[0:16, :], op=ALU.add), r=['ta', 'tb'], w=[('kr2_st', sl)])
                    op('sp', lambda e: [e.dma_start(out=d['kTn'][:, :, t0:t0 + T].rearrange("j p t -> p j t"), in_=kn_st[sl][:]),
                                        e.dma_start(out=d['kTr1'][:, t0:t0 + T], in_=kr1_st[sl][:]),
                                        e.dma_start(out=d['kTr2'][:, t0:t0 + T], in_=kr2_st[sl][:])],
                       r=[('kn_st', sl), ('kr1_st', sl), ('kr2_st', sl)], dsem=f'kst{sl}')
                    if i + 1 < NT:
                        prep(i + 1)
                    for j in range(4):
                        b = P.psum()
                        mm_group(PS(b), [(ckvg[:, k, j * 128:(j + 1) * 128], wv[:, k, :]) for k in range(2)],
                                 r=[('ckvg', m) for m in range(2)] + ['wkv'], w=[('ps', b)])
                        blk = 4 * i + j
                        op('act', lambda e, j=j, b=b, blk=blk: e.activation(out=VresBox[0][:, blk, :, 0:64],
                                                                           in_=PS(b).rearrange("p (h d) -> p h d", d=64),
                                                                           func=AF.Identity, scale=rtok[:, j:j + 1]),
                           r=[('ps', b), 'rtok'], w=[('V', blk)])
            P.barrier()

        def phaseB(l, s):
            S = SL[s]
            NQ = S // T
            NB = S // 128
            d = sc[s]
            with ExitStack() as st:
                def sb(name, shape, dt):
                    return st.enter_context(SBT(name, shape, dt))
                kt = [sb(f"kt{i}", [96, S], BF16) for i in range(2)]
                qt = [sb(f"qt{i}", [96, T], BF16) for i in range(3)]
                pt = [sb(f"pt{i}", [128, 3 * T], BF16) for i in range(3)]
                ost = [sb(f"ost{i}", [65, T], F32) for i in range(2)]
                OB_ = [6, 7]
                GS = 3
                kgroups = [(k0, min(GS, NB - k0)) for k0 in range(0, NB, GS)]
                NG = len(kgroups)

                groups = [(h, qb, G) for h in range(NH) for qb in range(NQ) for G in range(NG)]
                N = len(groups)

                def load_k(h):
                    sl = h % 2
                    op('sp', lambda e: [e.dma_start(out=kt[sl][0:64, :], in_=d['kTn'][h // 2, (h % 2) * 64:(h % 2) * 64 + 64, :]),
                                        e.dma_start(out=kt[sl][64:80, :], in_=d['kTr1'][:, :]),
                                        e.dma_start(out=kt[sl][80:96, :], in_=d['kTr2'][:, :])],
                       w=[('kt', sl)], dsem=f'kt{sl}')

                def load_q(g):
                    h, qb = divmod(g, NQ)
                    sl = g % 3
                    t0 = qb * T
                    op('sp', lambda e: [e.dma_start(out=qt[sl][0:64, :], in_=d['qTn'][h // 2, (h % 2) * 64:(h % 2) * 64 + 64, t0:t0 + T]),
                                        e.dma_start(out=qt[sl][64:80, :], in_=d['qTr1'][16 * h:16 * h + 16, t0:t0 + T]),
                                        e.dma_start(out=qt[sl][80:96, :], in_=d['qTr2'][16 * h:16 * h + 16, t0:t0 + T])],
                       w=[('qt', sl)], dsem=f'qt{sl}')

                def SW(ws, gs_):
                    return psall[:, ws * 1536:ws * 1536 + gs_ * 512]

                load_k(0)
                load_q(0)
                if NH * NQ > 1:
                    load_q(1)
                for n in range(N + 1):
                    if n < N:
                        h, qb, G = groups[n]
                        k0, gs_ = kgroups[G]
                        g = h * NQ + qb
                        if G == 0:
                            if qb == 0 and h + 1 < NH:
                                load_k(h + 1)
                            if g + 2 < NH * NQ:
                                load_q(g + 2)
                        ws = n % 2
                        ps_ = n % 3

                        def qk(e):
                            ins = None
                            for u_ in range(gs_):
                                kb = k0 + u_
                                ins = e.matmul(SW(ws, gs_)[:, u_ * 512:(u_ + 1) * 512], kt[h % 2][0:96, kb * 128:(kb + 1) * 128],
                                               qt[g % 3][0:96, :], start=True, stop=True)
                            return ins
                        tk_ = op('pe', qk, r=[('kt', h % 2), ('qt', g % 3)], w=[('psw', ws)])
                        if cast_queue and n % 64 == 32:
                            cast_queue.pop(0)(tk_)
                        op('act', lambda e: e.activation(out=pt[ps_][:, 0:gs_ * 512], in_=SW(ws, gs_), func=AF.Exp),
                           r=[('psw', ws)], w=[('pt', ps_)])
                    if n >= 1:
                        m_ = n - 1
                        h, qb, G = groups[m_]
                        k0, gs_ = kgroups[G]
                        g = h * NQ + qb
                        bo = OB_[g % 2]
                        ps_ = m_ % 3

                        def pv(e):
                            ins = None
                            for u_ in range(gs_):
                                kb = k0 + u_
                                ins = e.matmul(PS(bo)[0:65, :], VresBox[0][:, kb, h, :], pt[ps_][:, u_ * 512:(u_ + 1) * 512],
                                               start=(kb == 0), stop=(kb == NB - 1))
                            return ins
                        op('pe', pv, r=[('pt', ps_), 'Vones'] + [('V', k0 + u_) for u_ in range(gs_)], w=[('ps', bo)])
                        if G == NG - 1:
                            t0 = qb * T
                            o_ = ost[g % 2]
                            op('dve', lambda e: e.tensor_copy(out=o_[0:64, :], in_=PS(bo)[0:64, :]),
                               r=[('ps', bo)], w=[('ost', g % 2)])
                            op('dve', lambda e: e.reciprocal(out=o_[64:65, :], in_=PS(bo)[64:65, :]),
                               r=[('ps', bo), ('ost', g % 2)], w=[('ost', g % 2)])
                            op('sp', lambda e: [e.dma_start(out=d['oaU'][h * 64:(h + 1) * 64, t0:t0 + T], in_=o_[0:64, :]),
                                                e.dma_start(out=d['sums'][h:h + 1, t0:t0 + T], in_=o_[64:65, :])],
                               r=[('ost', g % 2)], dsem=f'ost{g % 2}')
            P.barrier()

        def phaseC1(l):
            with ExitStack() as st:
                def sb(name, shape, dt):
                    return st.enter_context(SBT(name, shape, dt))
                wC = sb("wC", [128, KC, 1536], BF16)
                wsn = sb("wsn", [128, 4, 128], BF16)
                wsT = sb("wsT", [128, 4, 128], BF16)
                kmT2 = [sb(f"kmT{i}", [128, 4, 256], BF16) for i in range(2)]
                vm2 = [sb(f"vm{i}", [128, 2, 512], BF16) for i in range(2)]
                bsb4 = sb("bsb4", [128, 4, T], F32)
                op('sp', lambda e: e.dma_start(out=wC[:], in_=WB['w_in'][l, :, 672:2208].rearrange("(c p) n -> p c n", p=128)),
                   w=['wC'], dsem='wA', after=[wtok['w_in1']])
                op('sp', lambda e: e.dma_start(out=wsn[:], in_=WB['w_s'][l].rearrange("g p q -> p g q")), w=['wsn'], dsem='wA', after=[wtok['w_s']])
                for g in range(4):
                    op('sp', lambda e, g=g: [e.dma_start(out=bsb4[:, g, j * 128:(j + 1) * 128], in_=W['b_s'][l, g, :].partition_broadcast(128))
                                             for j in range(4)], w=[('bsb4', g)], dsem='c0')
                gmg = load_bcast(st, "gmg", W['gm_ln_g'][l], 512)
                gmb = load_bcast(st, "gmb", W['gm_ln_b'][l], 512)
                with ExitStack() as st2:
                    wm = st2.enter_context(SBT("wm", [128, KC, 1024], BF16))
                    mf = st2.enter_context(SBT("mf", [128, 2, D], F32))
                    mb = st2.enter_context(SBT("mb", [128, 2, D], BF16))
                    mT = st2.enter_context(SBT("mT", [128, KC, 256], BF16))
                    op('sp', lambda e: e.dma_start(out=wm[:], in_=WB['w_mem_kv'][l].rearrange("(c p) n -> p c n", p=128)), w=['wm'], dsem='wA', after=[wtok['w_mem_kv']])
                    P.barrier()
                    b = P.psum()

                    def trs(e, b=b):
                        ins = None
                        for g in range(4):
                            ins = e.transpose(PSB16(b)[:, g * 128:(g + 1) * 128], wsn[:, g, :], identB[:])
                        return ins
                    op('pe', trs, r=['wsn', 'identB'], w=[('ps', b)])
                    op('dve', lambda e: e.tensor_copy(out=wsT[:].rearrange("p g q -> p (g q)"), in_=PSB16(b)[:, 0:512]), r=[('ps', b)], w=['wsT'])
                    for s in range(2):
                        op('sp', lambda e: e.dma_start(out=mf[:], in_=mem_in[s].rearrange("(j p) d -> p j d", p=128)), w=['mf'], dsem='memf')
                        op('dve', lambda e: e.tensor_copy(out=mb[:], in_=mf[:]), r=['mf'], w=['mb'])
                        for m in range(2):
                            b = P.psum()

                            def trm(e, m=m, b=b):
                                ins = None
                                for cc in range(4):
                                    c = 4 * m + cc
                                    for j in range(2):
                                        ins = e.transpose(PSB16(b)[:, cc * 256 + j * 128:cc * 256 + (j + 1) * 128],
                                                          mb[:, j, c * 128:(c + 1) * 128], identB[:])
                                return ins
                            op('pe', trm, r=['mb', 'identB'], w=[('ps', b)])
                            op('dve', lambda e, m=m, b=b: e.tensor_copy(out=mT[:, 4 * m:4 * m + 4, :].rearrange("p a t -> p (a t)"), in_=PSB16(b)),
                               r=[('ps', b)], w=[('mT', m)])
                        for h in range(4):
                            b = P.psum()
                            mm_group(PS(b)[:, 0:256], [(wm[:, k, h * 128:(h + 1) * 128], mT[:, k, :]) for k in range(KC)],
                                     r=['wm', ('mT', 0), ('mT', 1)], w=[('ps', b)])
                            op('dve', lambda e, h=h, b=b: e.tensor_copy(out=kmT2[s][:, h, :], in_=PS(b)[:, 0:256]), r=[('ps', b)], w=['kmT'])
                        for j in range(2):
                            b = P.psum()
                            mm_group(PS(b), [(mT[:, k, j * 128:(j + 1) * 128], wm[:, k, 512:1024]) for k in range(KC)],
                                     r=['wm', ('mT', 0), ('mT', 1)], w=[('ps', b)])
                            op('dve', lambda e, j=j, b=b: e.tensor_copy(out=vm2[s][:, j, :], in_=PS(b)), r=[('ps', b)], w=['vm'])
                    P.barrier()

                xT = [sb(f"xTc{i}", [128, KC, T], BF16) for i in range(2)]
                oaU = [sb(f"oaU{i}", [128, 4, T], F32) for i in range(2)]
                smb = [sb(f"smb{i}", [128, 4, T], F32) for i in range(2)]
                u2 = [sb(f"u{i}", [128, 4, T], F32) for i in range(2)]
                gv = sb("gv", [128, 4, T], F32)
                vb = sb("vb", [128, 4, T], BF16)
                stats = sb("stats", [128, 4, 1, 6], F32)
                mv = sb("mv", [128, 4, 2], F32)
                tmp4 = sb("tmp4", [128, 4], F32)
                rs4 = sb("rs4", [128, 4], F32)
                nmr4 = sb("nmr4", [128, 4], F32)
                qm = sb("qm", [128, 4, T], BF16)
                pm = [sb(f"pm{i}", [128, T], BF16) for i in range(2)]
                rsum = sb("rsum", [128, T], F32)
                tsp = sb("tsp", [128, T], F32)
                brst = [sb(f"brst{i}", [128, 12, T], BF16) for i in range(2)]

                tiles = [(s_, i_) for s_ in range(2) for i_ in range(SL[s_] // T)]

                def load(n_):
                    s_, i_ = tiles[n_]
                    d = sc[s_]
                    sl = n_ % 2
                    t0 = i_ * T
                    op('sp', lambda e: e.dma_start(out=xT[sl][:], in_=d['xT'][:, :, t0:t0 + T].rearrange("c p t -> p c t")),
                       w=[('xT', sl)], dsem=f'xT{sl}')
                    op('sp', lambda e: e.dma_start(out=oaU[sl][:], in_=d['oaU'][:, t0:t0 + T].rearrange("(c p) t -> p c t", p=128)),
                       w=[('oaU', sl)], dsem=f'oaU{sl}')
                    op('sp', lambda e: [e.dma_start(out=smb[sl][(hh % 2) * 64:(hh % 2) * 64 + 64, hh // 2, :],
                                                    in_=d['sums'][hh, t0:t0 + T].partition_broadcast(64)) for hh in range(NH)],
                       w=[('smb', sl)], dsem=f'smb{sl}')

                pend_mix = []

                def mk_mix(sl, d, t0):
                    bst = brst[sl]
                    u = u2[sl]

                    def run():
                        for g in range(4):
                            b = P.psum()

                            def mix(e, g=g, b=b):
                                ins = None
                                for j in range(4):
                                    ins = e.matmul(PS(b)[:, j * 128:(j + 1) * 128], vb[:, j, g * 128:(g + 1) * 128], wsT[:, g, :], start=True, stop=True)
                                return ins
                            op('pe', mix, r=[('vb', j) for j in range(4)] + ['wsT'], w=[('ps', b)])
                            op('dve', lambda e, g=g, b=b: e.tensor_tensor(out=tsp[:], in0=PS(b), in1=bsb4[:, g, :], op=ALU.add),
                               r=[('ps', b), ('bsb4', g)], w=['tsp'])
                            op('dve', lambda e, g=g: e.tensor_tensor(out=bst[:, 4 + g, :], in0=tsp[:], in1=u[:, g, :], op=ALU.mult),
                               r=['tsp', ('u', sl, g)], w=[('brst', sl, 4 + g)])

                        op('sp', lambda e: e.dma_start(out=d['brT'][:, :, t0:t0 + T].rearrange("c p t -> p c t"), in_=bst[:]),
                           r=[('brst', sl, c) for c in range(12)], dsem=f'brst{sl}')
                    return run

                load(0)
                for n_, (s, i) in enumerate(tiles):
                    d = sc[s]
                    sl = n_ % 2
                    t0 = i * T
                    if n_ + 1 < len(tiles):
                        load(n_ + 1)
                    xk = [('xT', sl)]
                    u = u2[sl]
                    bst = brst[sl]
                    for c in range(4):
                        b = P.psum()
                        mm_group(PS(b), [(wC[:, k, c * 128:(c + 1) * 128], xT[sl][:, k, :]) for k in range(KC)], r=xk + ['wC'], w=[('ps', b)])
                        op('act', lambda e, c=c, b=b: e.activation(out=u[:, c, :], in_=PS(b), func=AF.Gelu_apprx_tanh), r=[('ps', b)], w=[('u', sl, c)])
                    for j in range(4):
                        b = P.psum()
                        mm_group(PS(b), [(xT[sl][:, k, j * 128:(j + 1) * 128], wC[:, k, 512:1024]) for k in range(KC)], r=xk + ['wC'], w=[('ps', b)])
                        op('act', lambda e, j=j, b=b: e.activation(out=gv[:, j, :], in_=PS(b), func=AF.Gelu_apprx_tanh), r=[('ps', b)], w=[('gv', j)])
                    for h in range(4):
                        b = P.psum()
                        mm_group(PS(b), [(wC[:, k, 1024 + h * 128:1024 + (h + 1) * 128], xT[sl][:, k, :]) for k in range(KC)], r=xk + ['wC'], w=[('ps', b)])
                        op('act', lambda e, h=h, b=b: e.activation(out=qm[:, h, :], in_=PS(b), func=AF.Identity, scale=MEM_SCALE), r=[('ps', b)], w=[('qm', h)])
                    while pend_mix:
                        pend_mix.pop(0)()
                    ln_rows(gv, 4, 512, gmg, gmb, mv, tmp4, rs4, 'gv', LN_EPS, stats=stats, gb_eng='dve',
                            fin_out=lambda j: vb[:, j, :], fin_key='vb', nmr=nmr4)
                    for h in range(4):
                        bpv = P.psum()
                        bsm = P.psum()
                        for blk in range(2):
                            bs = P.psum()
                            mm_group(PS(bs), [(kmT2[s][:, h, blk * 128:(blk + 1) * 128], qm[:, h, :])], r=['kmT', ('qm', h)], w=[('ps', bs)])
                            pp = pm[blk]
                            op('act', lambda e, pp=pp, bs=bs: e.activation(out=pp[:], in_=PS(bs), func=AF.Exp), r=[('ps', bs)], w=[('pm', blk)])
                            mm_group(PS(bpv), [(vm2[s][:, blk, h * 128:(h + 1) * 128], pp[:])], r=['vm', ('pm', blk)], w=[('ps', bpv)],
                                     start=(blk == 0), stop=(blk == 1))
                            mm_group(PS(bsm), [(onesB[:], pp[:])], r=['onesB', ('pm', blk)], w=[('ps', bsm)],
                                     start=(blk == 0), stop=(blk == 1))
                        op('act', lambda e, bsm=bsm: e.activation(out=rsum[:], in_=PS(bsm), func=AF.Ln), r=[('ps', bsm)], w=['rsum'])
                        op('act', lambda e: e.activation(out=rsum[:], in_=rsum[:], func=AF.Exp, scale=-1.0), r=['rsum'], w=['rsum'])
                        op('dve', lambda e, h=h, bpv=bpv: e.tensor_tensor(out=bst[:, 8 + h, :], in0=PS(bpv), in1=rsum[:], op=ALU.mult),
                           r=[('ps', bpv), 'rsum'], w=[('brst', sl, 8 + h)])
                    for c in range(4):
                        op('pool', lambda e, c=c: e.tensor_tensor(out=bst[:, c, :], in0=oaU[sl][:, c, :], in1=smb[sl][:, c, :], op=ALU.mult),
                           r=[('oaU', sl), ('smb', sl)], w=[('brst', sl, c)])
                    pend_mix.append(mk_mix(sl, d, t0))
                while pend_mix:
                    pend_mix.pop(0)()
            P.barrier()

        def phaseC2(l, x_srcs):
            with ExitStack() as st:
                def sb(name, shape, dt):
                    return st.enter_context(SBT(name, shape, dt))
                wG = sb("wG", [128, KC, 3072], BF16)
                wbr = sb("wbr", [128, 12, D], BF16)
                wo = sb("wo", [128, KC, D], BF16)
                op('sp', lambda e: e.dma_start(out=wG[:], in_=WB['w_in'][l, :, 2208:5280].rearrange("(c p) n -> p c n", p=128)), w=['wG'], dsem='wA', after=[wtok['w_in2']])
                for bi, nm in enumerate(['w_br_mla', 'w_br_gmlp', 'w_br_mem']):
                    op('sp', lambda e, bi=bi, nm=nm: e.dma_start(out=wbr[:, 4 * bi:4 * bi + 4, :], in_=WB[nm][l].rearrange("(c p) n -> p c n", p=128)),
                       w=[('wbr', bi)], dsem='wA', after=[wtok['w_br_mla'], wtok['w_br_gmlp'], wtok['w_br_mem']])
                op('sp', lambda e: e.dma_start(out=wo[:], in_=WB['w_o'][l].rearrange("(c p) n -> p c n", p=128)), w=['wo'], dsem='wA', after=[wtok['w_o']])
                bg = load_cols(st, "bgc", W['b_gate'][l].rearrange("(c p) -> c p", p=128), 24, 'bg')
                g1 = load_bcast(st, "g1", W['ln1_g'][l], D)
                b1 = load_bcast(st, "b1", W['ln1_b'][l], D)
                finish_cols()

                xT = [sb(f"xTd{i}", [128, KC, T], BF16) for i in range(2)]
                br = [sb(f"brd{i}", [128, 12, T], BF16) for i in range(2)]
                gt = [sb(f"gt{i}", [128, T], BF16) for i in range(6)]
                t1 = [sb(f"t1_{i}", [128, T], F32) for i in range(3)]
                mg = sb("mg", [128, KC, T], BF16)
                r4 = [sb("r4_0", [128, 4, D], F32)] * 2
                yb = sb("yb", [128, 4, D], BF16)
                yTst = [sb("yTst0", [128, KC, T], BF16)] * 2
                stats = sb("stats2", [128, 4, 2, 6], F32)
                mv = sb("mv2", [128, 4, 2], F32)
                tmp4 = sb("tmp42", [128, 4], F32)
                rs4 = sb("rs42", [128, 4], F32)
                nmr4 = sb("nmr42", [128, 4], F32)

                tiles = [(s_, i_) for s_ in range(2) for i_ in range(SL[s_] // T)]

                def load(n_):
                    s_, i_ = tiles[n_]
                    d = sc[s_]
                    sl = n_ % 2
                    t0 = i_ * T
                    op('sp', lambda e: e.dma_start(out=xT[sl][:], in_=d['xT'][:, :, t0:t0 + T].rearrange("c p t -> p c t")), w=[('xT', sl)], dsem=f'xT{sl}')
                    op('sp', lambda e: e.dma_start(out=br[sl][:], in_=d['brT'][:, :, t0:t0 + T].rearrange("c p t -> p c t")), w=[('br', sl)], dsem=f'br{sl}')

                pend_tr = []

                def emit_tr():
                    while pend_tr:
                        d_, t0_ = pend_tr.pop(0)
                        for m in range(4):
                            b = P.psum()

                            def tr(e, m=m, b=b):
                                ins = None
                                for cc in range(2):
                                    c = 2 * m + cc
                                    for j in range(4):
                                        ins = e.transpose(PSB16(b)[:, cc * 512 + j * 128: cc * 512 + (j + 1) * 128],
                                                          yb[:, j, c * 128:(c + 1) * 128], identB[:])
                                return ins
                            op('pe', tr, r=[('yb', j) for j in range(4)] + ['identB'], w=[('ps', b)])
                            op('dve', lambda e, m=m, b=b: e.tensor_copy(out=yTst[0][:, 2 * m:2 * m + 2, :].rearrange("p a t -> p (a t)"), in_=PSB16(b)),
                               r=[('ps', b)], w=[('yTst', 0, m)])
                        op('sp', lambda e, t0_=t0_, d_=d_: e.dma_start(out=d_['yT'][:, :, t0_:t0_ + T].rearrange("c p t -> p c t"), in_=yTst[0][:]),
                           r=[('yTst', 0, m) for m in range(4)], dsem='yTst0')

                load(0)
                gi = 0
                for n_, (s, i) in enumerate(tiles):
                    d = sc[s]
                    x_src = x_srcs[s]
                    sl = n_ % 2
                    t0 = i * T
                    if n_ + 1 < len(tiles):
                        load(n_ + 1)
                    R = r4[sl]
                    op('sp', lambda e: e.dma_start(out=R[:], in_=x_src[t0:t0 + T, :].rearrange("(j p) d -> p j d", p=128)),
                       w=[(('r4', 0), j) for j in range(4)], dsem='r4ld')
                    for m in range(8):
                        gts = []
                        for bi in range(3):
                            b = P.psum()
                            col = (bi * 8 + m) * 128
                            mm_group(PS(b), [(wG[:, k, col:col + 128], xT[sl][:, k, :]) for k in range(KC)], r=[('xT', sl), 'wG'], w=[('ps', b)])
                            gsl = gi % 6
                            gi += 1
                            op('act', lambda e, b=b, gsl=gsl, bi=bi, m=m: e.activation(out=gt[gsl][:], in_=PS(b), func=AF.Sigmoid,
                                                                                    bias=bg[:, bi * 8 + m:bi * 8 + m + 1]),
                               r=[('ps', b), 'bgc'], w=[('gt', gsl)])
                            gts.append(gsl)
                        for bi in range(3):
                            b = P.psum()
                            mm_group(PS(b), [(wbr[:, 4 * bi + k, m * 128:(m + 1) * 128], br[sl][:, 4 * bi + k, :]) for k in range(4)],
                                     r=[('br', sl), ('wbr', bi)], w=[('ps', b)])
                            op('dve', lambda e, b=b, bi=bi, gsl=gts[bi]: e.tensor_tensor(out=t1[bi][:], in0=PS(b), in1=gt[gsl][:], op=ALU.mult),
                               r=[('ps', b), ('gt', gts[bi])], w=[('t1', bi)])
                        op('dve', lambda e: e.tensor_tensor(out=t1[0][:], in0=t1[0][:], in1=t1[1][:], op=ALU.add), r=[('t1', 0), ('t1', 1)], w=[('t1', 0)])
                        op('dve', lambda e, m=m: e.tensor_tensor(out=mg[:, m, :], in0=t1[0][:], in1=t1[2][:], op=ALU.add),
                           r=[('t1', 0), ('t1', 2)], w=[('mg', m)])
                    emit_tr()
                    for j in range(4):
                        for hf in range(2):
                            b = P.psum()
                            mm_group(PS(b), [(mg[:, k, j * 128:(j + 1) * 128], wo[:, k, hf * 512:(hf + 1) * 512]) for k in range(KC)],
                                     r=[('mg', k) for k in range(KC)] + ['wo'], w=[('ps', b)])
                            op('dve', lambda e, j=j, hf=hf, b=b: e.scalar_tensor_tensor(out=R[:, j, hf * 512:(hf + 1) * 512],
                                                                                       in0=R[:, j, hf * 512:(hf + 1) * 512], scalar=ALPHA,
                                                                                       in1=PS(b), op0=ALU.mult, op1=ALU.add),
                               r=[('ps', b), (('r4', 0), j)], w=[(('r4', 0), j)])
                    ln_rows(R, 4, D, g1, b1, mv, tmp4, rs4, ('r4', 0), LN_EPS, stats=stats, nmr=nmr4, dve_sub=(0, 1, 2, 3))
                    op('sp', lambda e, t0=t0: e.dma_start(out=d['y'][t0:t0 + T, :].rearrange("(j p) d -> p j d", p=128), in_=R[:]),
                       r=[(('r4', 0), j) for j in range(4)], dsem=f'yst{sl}')
                    for j in range(4):
                        op('pool', lambda e, j=j: e.tensor_copy(out=yb[:, j, :], in_=R[:, j, :]), r=[(('r4', 0), j)], w=[('yb', j)])
                    pend_tr.append((d, t0))
                emit_tr()
            P.barrier()

        def phaseD(l, hh, dsts):
            NP = NPAIR // 2
            with ExitStack() as st:
                def sb(name, shape, dt):
                    return st.enter_context(SBT(name, shape, dt))
                wfa = sb("wfa", [128, KC, NP * 128], BF16)
                wfg = sb("wfg", [128, KC, NP * 128], BF16)
                wfo = sb("wfo", [128, NP, D], BF16)
                a0 = hh * NP * 128
                op('sp', lambda e: e.dma_start(out=wfa[:], in_=WB['w_ffn_in'][l, :, a0:a0 + NP * 128].rearrange("(c p) n -> p c n", p=128)), w=['wfa'], dsem='wA', after=[wtok['w_ffn_in']])
                op('sp', lambda e: e.dma_start(out=wfg[:], in_=WB['w_ffn_in'][l, :, DFF + a0:DFF + a0 + NP * 128].rearrange("(c p) n -> p c n", p=128)), w=['wfg'], dsem='wA', after=[wtok['w_ffn_in']])
                op('sp', lambda e: e.dma_start(out=wfo[:], in_=WB['w_ffn_out'][l, a0:a0 + NP * 128, :].rearrange("(c p) n -> p c n", p=128)), w=['wfo'], dsem='wA', after=[wtok['w_ffn_out']])
                cwa = [load_cols(st, f"cwa{t_}", W['conv_w'][l, t_, a0:a0 + NP * 128].rearrange("(c p) -> c p", p=128), NP, 'cw') for t_ in range(3)]
                cwg = [load_cols(st, f"cwg{t_}", W['conv_w'][l, t_, DFF + a0:DFF + a0 + NP * 128].rearrange("(c p) -> c p", p=128), NP, 'cw') for t_ in range(3)]
                cba = load_cols(st, "cba", W['conv_b'][l, a0:a0 + NP * 128].rearrange("(c p) -> c p", p=128), NP, 'cb')
                cbg = load_cols(st, "cbg", W['conv_b'][l, DFF + a0:DFF + a0 + NP * 128].rearrange("(c p) -> c p", p=128), NP, 'cb')
                g2 = b2 = None
                if hh == 1:
                    g2 = load_bcast(st, "g2", W['ln2_g'][l], D)
                    b2 = load_bcast(st, "b2", W['ln2_b'][l], D)
                finish_cols()

                yT = [sb(f"yTe{i}", [128, KC, T], BF16) for i in range(2)]
                E = [sb(f"E{i}", [128, T + 2], F32) for i in range(4)]
                acc = [sb(f"acc{i}", [128, T], F32) for i in range(4)]
                sg = [sb(f"sg{i}", [128, T], F32) for i in range(2)]
                sv = sb("sv", [128, 2 * NP, 2], F32)
                actT = [sb(f"actT{i}", [128, NP, T], BF16) for i in range(2)]
                R4 = [sb(f"R4_{i}", [128, 4, D], F32) for i in range(2)]
                Pt = [sb(f"Pt{i}", [128, 4, D], F32) for i in range(2)]
                stats = sb("stats3", [128, 4, 2, 6], F32)
                mv = sb("mv3", [128, 4, 2], F32)
                tmp4 = sb("tmp43", [128, 4], F32)
                rs4 = sb("rs43", [128, 4], F32)
                nmr4 = sb("nmr43", [128, 4], F32)
                for i_ in range(4):
                    op('dve', lambda e, i_=i_: e.memset(E[i_][:], 0.0), w=[('E', i_)])

                tiles = []
                for s_ in range(2):
                    nt_ = SL[s_] // T
                    tiles += [(s_, i_, False) for i_ in range(nt_)] + [(s_, nt_, True)]

                def load(n_):
                    s_, i_, fl_ = tiles[n_]
                    if fl_:
                        return
                    sl_ = n_ % 2
                    t0_ = i_ * T
                    op('sp', lambda e: e.dma_start(out=yT[sl_][:], in_=sc[s_]['yT'][:, :, t0_:t0_ + T].rearrange("c p t -> p c t")),
                       w=[('yT', sl_)], dsem=f'yT{sl_}')

                def win_rows(i, j):
                    w0 = i * T - 1 + 128 * j
                    if w0 < 0:
                        return 1, 0, 127
                    return 0, w0, 128

                def make_out(n, s, i, flush, sl):
                    S = SL[s]
                    d = sc[s]
                    dst = dsts[s]
                    nsub = 1 if flush else 4
                    A_ = actT[sl]
                    out = []
                    for j in range(nsub):
                        nrow = 1 if flush else 128
                        c0 = 0 if flush else j * 128
                        for hf in range(2):
                            def grp(j=j, hf=hf, nrow=nrow, c0=c0):
                                b = P.psum()
                                mm_group(PS(b)[0:nrow, :], [(A_[:, k, c0:c0 + nrow], wfo[:, k, hf * 512:(hf + 1) * 512]) for k in range(NP)],
                                         r=[('actT', sl, k) for k in range(NP)], w=[('ps', b)])
                                if hh == 0:
                                    op('act', lambda e: e.activation(out=Pt[sl][0:nrow, j, hf * 512:(hf + 1) * 512], in_=PS(b)[0:nrow, :], func=AF.Copy),
                                       r=[('ps', b)], w=[('Pt', sl)])
                                    return None

                                def evac():
                                    op('dve', lambda e: e.tensor_tensor(out=Pt[sl][0:nrow, j, hf * 512:(hf + 1) * 512],
                                                                        in0=PS(b)[0:nrow, :], in1=Pt[sl][0:nrow, j, hf * 512:(hf + 1) * 512], op=ALU.add),
                                       r=[('ps', b), ('Pt', sl)], w=[('Pt', sl)])
                                    op('dve', lambda e: e.scalar_tensor_tensor(out=R4[sl][0:nrow, j, hf * 512:(hf + 1) * 512],
                                                                               in0=R4[sl][0:nrow, j, hf * 512:(hf + 1) * 512], scalar=ALPHA,
                                                                               in1=Pt[sl][0:nrow, j, hf * 512:(hf + 1) * 512], op0=ALU.mult, op1=ALU.add),
                                       r=[('Pt', sl), (('R4', sl), j)], w=[(('R4', sl), j)])
                                return evac
                            out.append(grp)

                    def fin():
                        if hh == 0:
                            def stp(e):
                                if flush:
                                    return [e.dma_start(out=d['part'][S:S + 1, :], in_=Pt[sl][0:1, 0, :])]
                                t0 = i * T
                                return [e.dma_start(out=d['part'][t0:t0 + T, :].rearrange("(j p) d -> p j d", p=128), in_=Pt[sl][:])]
                            op('sp', stp, r=[('Pt', sl)], dsem=f'Ptst{sl}')
                        else:
                            ln_rows(R4[sl], nsub, D, g2, b2, mv, tmp4, rs4, ('R4', sl), LN_EPS, stats=stats,
                                    prow=(slice(0, 1) if flush else None), nmr=nmr4, dve_sub=(0, 1, 2, 3))

                            def sto(e):
                                if flush:
                                    return [e.dma_start(out=dst[S - 1:S, :], in_=R4[sl][0:1, 0, :])]
                                o_ = []
                                for j in range(4):
                                    p0, tk, n_ = win_rows(i, j)
                                    o_.append(e.dma_start(out=dst[tk:tk + n_, :], in_=R4[sl][p0:p0 + n_, j, :]))
                                return o_
                            op('sp', sto, r=[(('R4', sl), j) for j in range(4)], dsem=f'ost{sl}')
                    out.append(fin)
                    return out

                ei = 0
                pending = []
                late_ev = []

                def drain_pending():
                    while pending:
                        if len(pending) == 1:
                            while late_ev:
                                late_ev.pop(0)()
                        ev_ = pending.pop(0)()
                        if ev_ is not None:
                            late_ev.append(ev_)
                    while late_ev:
                        late_ev.pop(0)()

                load(0)
                for n, (s, i, flush) in enumerate(tiles):
                    S = SL[s]
                    d = sc[s]
                    sl = n % 2
                    WC = 1 if flush else T
                    nsub = 1 if flush else 4
                    if i == 0:
                        op('dve', lambda e: e.memset(sv[:], 0.0), w=['sv'])
                    if n + 1 < len(tiles):
                        load(n + 1)
                    if hh == 1:
                        def ldw(e):
                            o_ = []
                            for j in range(nsub):
                                if flush:
                                    o_.append(e.dma_start(out=R4[sl][0:1, 0, :], in_=d['y'][S - 1:S, :]))
                                    o_.append(e.dma_start(out=Pt[sl][0:1, 0, :], in_=d['part'][S:S + 1, :]))
                                else:
                                    p0, tk, n_ = win_rows(i, j)
                                    o_.append(e.dma_start(out=R4[sl][p0:p0 + n_, j, :], in_=d['y'][tk:tk + n_, :]))
                                    o_.append(e.dma_start(out=Pt[sl][p0:p0 + n_, j, :], in_=d['part'][tk + 1:tk + 1 + n_, :]))
                            return o_
                        op('sp', ldw, w=[(('R4', sl), j) for j in range(4)] + [('Pt', sl)], dsem=f'R4{sl}')
                    prev_s2 = None
                    for pj in range(NP):
                        accs = []
                        for ag in range(2):
                            wsrc = wfa if ag == 0 else wfg
                            ch = ag * NP + pj
                            cw = cwa if ag == 0 else cwg
                            cb = cba if ag == 0 else cbg
                            esl = ei % 4
                            ei += 1
                            Et = E[esl]
                            At = acc[esl]
                            op('pool', lambda e: e.tensor_copy(out=Et[:, 0:2], in_=sv[:, ch, :]), r=['sv'], w=[('E', esl)])
                            if not flush:
                                b = P.psum()
                                mm_group(PS(b), [(wsrc[:, k, pj * 128:(pj + 1) * 128], yT[sl][:, k, :]) for k in range(KC)],
                                         r=[('yT', sl)], w=[('ps', b)])
                                op('act', lambda e: e.activation(out=Et[:, 2:T + 2], in_=PS(b), func=AF.Copy), r=[('ps', b)], w=[('E', esl)])
                                op('pool', lambda e: e.tensor_copy(out=sv[:, ch, :], in_=Et[:, T:T + 2]), r=[('E', esl)], w=['sv'])
                            else:
                                op('pool', lambda e: e.memset(Et[:, 2:3], 0.0), w=[('E', esl)])
                            op('act', lambda e: e.activation(out=At[:, 0:WC], in_=Et[:, 1:1 + WC], func=AF.Identity,
                                                             bias=cb[:, pj:pj + 1], scale=cw[1][:, pj:pj + 1]),
                               r=[('E', esl)], w=[('acc', esl)])
                            op('dve', lambda e: e.scalar_tensor_tensor(out=At[:, 0:WC], in0=Et[:, 0:WC], scalar=cw[0][:, pj:pj + 1],
                                                                       in1=At[:, 0:WC], op0=ALU.mult, op1=ALU.add),
                               r=[('E', esl), ('acc', esl)], w=[('acc', esl)])
                            op('dve', lambda e: e.scalar_tensor_tensor(out=At[:, 0:WC], in0=Et[:, 2:2 + WC], scalar=cw[2][:, pj:pj + 1],
                                                                       in1=At[:, 0:WC], op0=ALU.mult, op1=ALU.add),
                               r=[('E', esl), ('acc', esl)], w=[('acc', esl)])
                            accs.append((At, esl))
                        def stage2(pj=pj, accs=accs):
                            ssl = pj % 2
                            op('act', lambda e: e.activation(out=sg[ssl][:, 0:WC], in_=accs[0][0][:, 0:WC], func=AF.Silu),
                               r=[('acc', accs[0][1])], w=[('sg', ssl)])
                            op('dve', lambda e: e.tensor_tensor(out=actT[sl][:, pj, 0:WC], in0=sg[ssl][:, 0:WC], in1=accs[1][0][:, 0:WC], op=ALU.mult),
                               r=[('sg', ssl), ('acc', accs[1][1])], w=[('actT', sl, pj)])
                        if prev_s2 is not None:
                            prev_s2()
                        prev_s2 = stage2
                        if late_ev:
                            late_ev.pop(0)()
                        if len(pending) > 1:
                            ev_ = pending.pop(0)()
                            if ev_ is not None:
                                late_ev.append(ev_)
                    prev_s2()
                    prev_s2 = None
                    drain_pending()
                    pending = make_out(n, s, i, flush, sl)
                drain_pending()
            P.barrier()

        for l in range(L):
            x_srcs = [x_in[s_] if l == 0 else sc[s_]['x1'] for s_ in range(2)]
            dsts = [sc[s_]['x1'] if l == 0 else y_out[s_] for s_ in range(2)]
            for s in range(2):
                with SBT("Vres", [128, SL[s] // 128, NH, 65], BF16) as Vres_:
                    VresBox[0] = Vres_
                    op('dve', lambda e: e.memset(Vres_[:, :, :, 64:65], 1.0), w=['Vones'])
                    phaseA(l, s, x_srcs[s])
                    phaseB(l, s)
                    if l == 0 and s == 1:
                        cast_rest()
            phaseC1(l)
            phaseC2(l, x_srcs)
            phaseD(l, 0, dsts)
            phaseD(l, 1, dsts)
        P.barrier(final=True)
    return nc


def _consts():
    identf = np.eye(128, dtype=np.float32)
    pos = np.arange(8192, dtype=np.float32)
    inv = (np.float32(10000.0) ** (-np.arange(0, 32, 2, dtype=np.float32) / np.float32(32))).astype(np.float32)
    ang = (pos[:, None] * inv[None, :]).astype(np.float32)
    c = np.cos(ang).astype(np.float32).T
    s_ = np.sin(ang).astype(np.float32).T
    return identf, np.ascontiguousarray(np.tile(c, (8, 1))), np.ascontiguousarray(np.tile(s_, (8, 1)))


_NC_CACHE = {}


def run(inputs, SP, SS, debug=False, trace=False):
    key = (SP, SS, debug)
    if key not in _NC_CACHE:
        _NC_CACHE[key] = build_nc(SP, SS, debug)
    nc = _NC_CACHE[key]
    identf, rc, rs = _consts()
    wd = {n: np.ascontiguousarray(np.asarray(inputs[n], dtype=np.float32)) for n in WNAMES}
    in_maps = []
    for b in range(8):
        m = dict(wd)
        m['xp'] = np.ascontiguousarray(inputs['x_prompt'][b])
        m['xs'] = np.ascontiguousarray(inputs['x_sample'][b])
        m['memp'] = np.ascontiguousarray(inputs['mem_prompt'][b])
        m['mems'] = np.ascontiguousarray(inputs['mem_sample'][b])
        m['identf'] = identf
        m['ropeC'] = rc
        m['ropeS'] = rs
        in_maps.append(m)
    res = run_bass_kernel_spmd(nc, in_maps, core_ids=list(range(8)), trace=trace)
    return res


def kernel(**inputs):
    inputs = {k: np.asarray(v) for k, v in inputs.items()}
    SP = inputs['x_prompt'].shape[1]
    SS = inputs['x_sample'].shape[1]
    res = run(inputs, SP, SS)
    yp = np.stack([res.results[b]['yp'] for b in range(8)], axis=0).astype(np.float32)
    ys = np.stack([res.results[b]['ys'] for b in range(8)], axis=0).astype(np.float32)
    return (yp, ys)
```

```python
import math
from contextlib import ExitStack

import numpy as np
import concourse.bass as bass
import concourse.mybir as mybir
from concourse.bass_utils import run_bass_kernel_spmd

F32 = mybir.dt.float32
BF16 = mybir.dt.bfloat16
AF = mybir.ActivationFunctionType
ALU = mybir.AluOpType

D = 1024
KC = 8
T = 512
L = 2
NH = 8
QL = 384
KVL = 256
DFF = 2816
NPAIR = 22
INW = 5280
ALPHA = (2 * L) ** 0.25
LN_EPS = 1e-5
RMS_EPS = 1e-6
SM_SCALE = 96 ** -0.5
MEM_SCALE = 128 ** -0.5

WNAMES = ['w_in', 'g_q', 'w_uq', 'g_kv', 'w_ukv', 'gm_ln_g', 'gm_ln_b', 'w_s', 'b_s', 'w_mem_kv',
          'b_gate', 'w_br_mla', 'w_br_gmlp', 'w_br_mem', 'w_o', 'ln1_g', 'ln1_b',
          'w_ffn_in', 'conv_w', 'conv_b', 'w_ffn_out', 'ln2_g', 'ln2_b']
WSHAPES = {
    'w_in': [L, D, INW], 'g_q': [L, QL], 'w_uq': [L, QL, 768], 'g_kv': [L, KVL], 'w_ukv': [L, KVL, 1024],
    'gm_ln_g': [L, 512], 'gm_ln_b': [L, 512], 'w_s': [L, 4, 128, 128], 'b_s': [L, 4, 128],
    'w_mem_kv': [L, D, 1024], 'b_gate': [L, 3072], 'w_br_mla': [L, 512, D], 'w_br_gmlp': [L, 512, D],
    'w_br_mem': [L, 512, D], 'w_o': [L, D, D], 'ln1_g': [L, D], 'ln1_b': [L, D],
    'w_ffn_in': [L, D, 2 * DFF], 'conv_w': [L, 3, 2 * DFF], 'conv_b': [L, 2 * DFF],
    'w_ffn_out': [L, DFF, D], 'ln2_g': [L, D], 'ln2_b': [L, D],
}
BFW = ['w_in', 'w_uq', 'w_ukv', 'w_s', 'w_mem_kv', 'w_br_mla', 'w_br_gmlp', 'w_br_mem', 'w_o',
       'w_ffn_in', 'w_ffn_out']


class _FirstWait:
    def __init__(self, eng, tok):
        self._eng = eng
        self._tok = tok
        self._done = False

    def __getattr__(self, name):
        attr = getattr(self._eng, name)
        if self._done or not callable(attr):
            return attr

        def wrapped(*a, **k):
            r = attr(*a, **k)
            if not self._done and hasattr(r, '_wait_ge'):
                r._wait_ge(self._tok[1], self._tok[2])
                self._done = True
            return r
        return wrapped


class Prog:
    def __init__(self, nc):
        self.nc = nc
        self.eng = {'pe': nc.tensor, 'act': nc.scalar, 'dve': nc.vector, 'pool': nc.gpsimd, 'sp': nc.sync}
        self.esem = {e: nc.alloc_semaphore("sem_" + e) for e in ('pe', 'act', 'dve', 'pool')}
        self.ecnt = {e: 0 for e in self.esem}
        self.dsems = {}
        self.last_w = {}
        self.readers = {}
        self.waited = {e: {} for e in self.eng}
        self.rr = 0

    def _wait(self, e, tok):
        name, sem, val = tok
        if self.waited[e].get(name, 0) >= val:
            return
        self.waited[e][name] = val
        self.eng[e].wait_ge(sem, val)

    def op(self, e, fn, r=(), w=(), dsem=None, after=()):
        toks = [t for t in after if t is not None]
        for x in r:
            t = self.last_w.get(x)
            if t is not None:
                toks.append(t)
        for x in w:
            t = self.last_w.get(x)
            if t is not None:
                toks.append(t)
            toks.extend(self.readers.get(x, {}).values())
        need = []
        seen = {}
        for t in toks:
            if e == 'pe' and t[0] == 'pe':
                continue
            if self.waited[e].get(t[0], 0) >= t[2]:
                continue
            if t[0] not in seen or seen[t[0]][2] < t[2]:
                seen[t[0]] = t
        need = list(seen.values())
        for t in need[:-1]:
            self._wait(e, t)
        if need:
            last = need[-1]
            self.waited[e][last[0]] = last[2]
            ins = fn(_FirstWait(self.eng[e], last))
        else:
            ins = fn(self.eng[e])
        if dsem is not None:
            if not isinstance(ins, (list, tuple)):
                ins = [ins]
            ent = self.dsems.get(dsem)
            if ent is None:
                ent = [self.nc.alloc_semaphore("d_" + dsem), 0]
                self.dsems[dsem] = ent
            for i_ in ins:
                ent[1] += 16
                i_.then_inc(ent[0], 16)
            tok = ("d_" + dsem, ent[0], ent[1])
        else:
            self.ecnt[e] += 1
            ins.then_inc(self.esem[e], 1)
            tok = (e, self.esem[e], self.ecnt[e])
        for x in w:
            self.last_w[x] = tok
            self.readers[x] = {}
        for x in r:
            d = self.readers.setdefault(x, {})
            o = d.get(tok[0])
            if o is None or o[2] < tok[2]:
                d[tok[0]] = tok
        return tok

    def barrier(self, final=False):
        toks = [(e, self.esem[e], self.ecnt[e]) for e in self.esem if self.ecnt[e] > 0]
        toks += [("d_" + k, v[0], v[1]) for k, v in self.dsems.items() if v[1] > 0 and (final or not k.startswith('wc_'))]
        for e in self.eng:
            for t in toks:
                self._wait(e, t)
        self.last_w = {}
        self.readers = {}

    def psum(self):
        b = self.rr
        self.rr = (self.rr + 1) % 8
        return b


def build_nc(SP, SS, debug=False):
    nc = bass.Bass("TRN2", target_bir_lowering=False)
    SL = [SP, SS]
    SMAX = max(SL)
    NBMAX = SMAX // 128

    def din(name, shape, dt=F32):
        return nc.dram_tensor(name, shape, dt, kind="ExternalInput").ap()

    def dint(name, shape, dt):
        return nc.dram_tensor(name, shape, dt, kind="ExternalOutput" if debug else "Internal").ap()

    x_in = [din("xp", [SP, D]), din("xs", [SS, D])]
    mem_in = [din("memp", [256, D]), din("mems", [256, D])]
    W = {n: din(n, WSHAPES[n]) for n in WNAMES}
    identf = din("identf", [128, 128])
    ropeC = din("ropeC", [128, 8192])
    ropeS = din("ropeS", [128, 8192])
    y_out = [nc.dram_tensor("yp", [SP, D], F32, kind="ExternalOutput").ap(),
             nc.dram_tensor("ys", [SS, D], F32, kind="ExternalOutput").ap()]

    WB = {n: nc.dram_tensor(n + "_bf", WSHAPES[n], BF16, kind="Internal").ap() for n in BFW}
    sc = []
    for s in range(2):
        S = SL[s]
        sc.append(dict(
            xT=dint(f"xT{s}", [8, 128, S], BF16),
            qTn=dint(f"qTn{s}", [4, 128, S], BF16), qTr1=dint(f"qTr1{s}", [128, S], BF16),
            qTr2=dint(f"qTr2{s}", [128, S], BF16),
            kTn=dint(f"kTn{s}", [4, 128, S], BF16), kTr1=dint(f"kTr1{s}", [16, S], BF16),
            kTr2=dint(f"kTr2{s}", [16, S], BF16),
            oaU=dint(f"oaU{s}", [512, S], F32), sums=dint(f"sums{s}", [8, S], F32),
            brT=dint(f"brT{s}", [12, 128, S], BF16),
            y=dint(f"y{s}", [S, D], F32), yT=dint(f"yT{s}", [8, 128, S], BF16),
            part=dint(f"part{s}", [S + 1, D], F32),
            x1=dint(f"x1{s}", [S, D], F32),
        ))

    P = Prog(nc)
    op = P.op
    _uid = [0]

    def SBT(name, shape, dt):
        _uid[0] += 1
        return nc.sbuf_tensor(f"{name}_u{_uid[0]}", shape, dt)

    with ExitStack() as gs:
        psall = gs.enter_context(nc.psum_tensor("psall", [128, 4096], F32))
        identF = gs.enter_context(nc.sbuf_tensor("identF", [128, 128], F32))
        identB = gs.enter_context(nc.sbuf_tensor("identB", [128, 128], BF16))
        onesB = gs.enter_context(nc.sbuf_tensor("onesB", [128, 128], BF16))
        VresBox = [None]

        def PS(b):
            return psall[:, b * 512:(b + 1) * 512]

        def PSB16(b):
            return PS(b).bitcast(BF16)

        def PSW(g):
            return psall[:, g * 1024:(g + 1) * 1024]

        op('sp', lambda e: e.dma_start(out=identF[:], in_=identf[:, :]), w=['identF'], dsem='c0')
        op('dve', lambda e: e.tensor_copy(out=identB[:], in_=identF[:]), r=['identF'], w=['identB'])
        op('dve', lambda e: e.memset(onesB[:], 1.0), w=['onesB'])
        P.barrier()
        wtok = {}
        w_in_src = W['w_in'].rearrange("l r n -> (l r) n")
        w_in_dst = WB['w_in'].rearrange("l r n -> (l r) n")
        W_IN_GROUPS = [(0, 672), (672, 2208), (2208, INW)]

        cast_queue = []

        def cast_w_in(gi_, queue=False):
            c0_, c1_ = W_IN_GROUPS[gi_]
            for i in range(2 * L):
                def piece(after=None, i=i):
                    wtok['w_in%d' % gi_] = op('pool', lambda e: e.dma_start(out=w_in_dst[i * 512:(i + 1) * 512, c0_:c1_],
                                                                           in_=w_in_src[i * 512:(i + 1) * 512, c0_:c1_]),
                                              w=[('wb', 'w_in', gi_, i)], dsem='wc_w_in%d' % gi_, after=[after])
                if queue:
                    cast_queue.append(piece)
                else:
                    piece()

        def cast_w(n, queue=False):
            src, dst = W[n], WB[n]
            if n == 'w_s':
                src = src.rearrange("l g p q -> (l g p) q")
                dst = dst.rearrange("l g p q -> (l g p) q")
            else:
                src = src.rearrange("l r n -> (l r) n")
                dst = dst.rearrange("l r n -> (l r) n")
            R = src.shape[0]
            nblk = max(1, R // 512)
            rb = R // nblk
            for i in range(nblk):
                def piece(after=None, i=i):
                    wtok[n] = op('pool', lambda e: e.dma_start(out=dst[i * rb:(i + 1) * rb, :], in_=src[i * rb:(i + 1) * rb, :]),
                                 w=[('wb', n, i)], dsem='wc_' + n, after=[after])
                if queue:
                    cast_queue.append(piece)
                else:
                    piece()

        cast_w_in(0)
        cast_w('w_uq')
        cast_w('w_ukv')
        for n in BFW:
            if n not in ('w_in', 'w_uq', 'w_ukv'):
                wtok[n] = None
        wtok['w_in1'] = wtok['w_in2'] = None
        cast_w_in(1, queue=True)
        for n in ('w_s', 'w_mem_kv'):
            cast_w(n, queue=True)
        cast_w_in(2, queue=True)
        for n in BFW:
            if n not in ('w_in', 'w_uq', 'w_ukv', 'w_s', 'w_mem_kv'):
                cast_w(n, queue=True)

        def cast_rest():
            while cast_queue:
                cast_queue.pop(0)()

        def rstd_ops(out_ap, in_ap, tmp_ap, scale_in, eps, lnmul, rkeys, wkey, tmpkey):
            op('act', lambda e: e.activation(out=tmp_ap, in_=in_ap, func=AF.Ln, bias=eps, scale=scale_in),
               r=rkeys, w=[tmpkey])
            op('act', lambda e: e.activation(out=out_ap, in_=tmp_ap, func=AF.Exp, bias=lnmul, scale=-0.5),
               r=[tmpkey], w=[wkey])

        def mm_group(ps_ap, pairs, r, w, start=True, stop=True):
            def fn(e):
                ins = None
                n = len(pairs)
                for i, (lh, rh) in enumerate(pairs):
                    ins = e.matmul(ps_ap, lh, rh, start=(start and i == 0), stop=(stop and i == n - 1))
                return ins
            return op('pe', fn, r=r, w=w)

        pending_cols = []

        def load_cols(st, name, rows_ap, nrows, tag):
            raw = st.enter_context(SBT(name + "_raw", [nrows, 128], F32))
            out = st.enter_context(SBT(name, [128, nrows], F32))
            op('sp', lambda e: e.dma_start(out=raw[:], in_=rows_ap), w=[name + '_raw'], dsem='c0')
            pending_cols.append((name, raw, out, nrows))
            return out

        def finish_cols():
            P.barrier()
            for (name, raw, out, nrows) in pending_cols:
                b = P.psum()
                mm_group(PS(b)[:, 0:nrows], [(raw[:], identF[0:nrows, 0:nrows])], r=[], w=[('ps', b)])
                op('dve', lambda e, out=out, b=b, nrows=nrows: e.tensor_copy(out=out[:], in_=PS(b)[:, 0:nrows]), r=[('ps', b)], w=[name])
            del pending_cols[:]
            P.barrier()

        def load_bcast(st, name, vec_ap, n):
            t = st.enter_context(SBT(name, [128, n], F32))
            op('sp', lambda e: e.dma_start(out=t[:], in_=vec_ap.partition_broadcast(128)), w=[name], dsem='c0')
            return t

        def ln_rows(buf, nsub, n, g_bc, b_bc, mv, tmp4, rs4, key, eps, prow=None, var_scale=1.0, out_scale=1.0,
                    stats=None, pre_r=(), gb_eng='pool', fin_out=None, fin_key=None, nmr=None, dve_sub=()):
            rows = slice(0, 128) if prow is None else prow
            nch = n // 512
            for j in range(nsub):
                for c in range(nch):
                    op('dve', lambda e, j=j, c=c: e.bn_stats(out=stats[rows, j, c, :], in_=buf[rows, j, c * 512:(c + 1) * 512]),
                       r=[(key, j)] + list(pre_r), w=[(key, 'st', j, c)])
                op('dve', lambda e, j=j: e.bn_aggr(out=mv[rows, j, :], in_=stats[rows, j, :, :]),
                   r=[(key, 'st', j, c) for c in range(nch)], w=[(key, 'mv', j)])
            rstd_ops(rs4[rows, 0:nsub], mv[rows, 0:nsub, 1], tmp4[rows, 0:nsub], var_scale, eps,
                     math.log(out_scale) if out_scale != 1.0 else 0.0,
                     [(key, 'mv', j) for j in range(nsub)], (key, 'rs'), (key, 'tmp4'))
            op('dve', lambda e: e.scalar_tensor_tensor(out=nmr[rows, 0:nsub], in0=mv[rows, 0:nsub, 0], scalar=-1.0, in1=rs4[rows, 0:nsub],
                                                       op0=ALU.mult, op1=ALU.mult),
               r=[(key, 'rs')] + [(key, 'mv', j) for j in range(nsub)], w=[(key, 'nmr')])
            for j in range(nsub):
                if j in dve_sub:
                    op('dve', lambda e, j=j: e.scalar_tensor_tensor(out=buf[rows, j, :], in0=buf[rows, j, :], scalar=mv[rows, j, 0:1],
                                                                    in1=g_bc[rows, :], op0=ALU.subtract, op1=ALU.mult),
                       r=[(key, j), (key, 'mv', j)], w=[(key, j)])
                    op('dve', lambda e, j=j: e.scalar_tensor_tensor(out=(buf[rows, j, :] if fin_out is None else fin_out(j)), in0=buf[rows, j, :],
                                                                    scalar=rs4[rows, j:j + 1], in1=b_bc[rows, :], op0=ALU.mult, op1=ALU.add),
                       r=[(key, j), (key, 'rs')], w=[(key, j) if fin_out is None else (fin_key, j)])
                    continue
                op('act', lambda e, j=j: e.activation(out=buf[rows, j, :], in_=buf[rows, j, :], func=AF.Identity,
                                                      bias=nmr[rows, j:j + 1], scale=rs4[rows, j:j + 1]),
                   r=[(key, j), (key, 'nmr'), (key, 'rs')], w=[(key, j)])
                op(gb_eng, lambda e, j=j: e.tensor_tensor(out=buf[rows, j, :], in0=buf[rows, j, :], in1=g_bc[rows, :], op=ALU.mult),
                   r=[(key, j)], w=[(key, j)])
                if fin_out is None:
                    op(gb_eng, lambda e, j=j: e.tensor_tensor(out=buf[rows, j, :], in0=buf[rows, j, :], in1=b_bc[rows, :], op=ALU.add),
                       r=[(key, j)], w=[(key, j)])
                else:
                    op(gb_eng, lambda e, j=j: e.tensor_tensor(out=fin_out(j), in0=buf[rows, j, :], in1=b_bc[rows, :], op=ALU.add),
                       r=[(key, j)], w=[(fin_key, j)])

        def phaseA(l, s, x_src):
            S = SL[s]
            NT = S // T
            d = sc[s]
            with ExitStack() as st:
                def sb(name, shape, dt):
                    return st.enter_context(SBT(name, shape, dt))
                wA = sb("wA", [128, KC, 672], BF16)
                wqn = sb("wqn", [128, 3, 512], BF16)
                wqr1 = sb("wqr1", [128, 3, 128], BF16)
                wqr2 = sb("wqr2", [128, 3, 128], BF16)
                wkn = sb("wkn", [128, 2, 512], BF16)
                wv = sb("wv", [128, 2, 512], BF16)
                with nc.allow_non_contiguous_dma(reason="one-time weight column gathers"):
                    op('sp', lambda e: e.dma_start(out=wA[:], in_=WB['w_in'][l, :, 0:672].rearrange("(c p) n -> p c n", p=128)),
                       w=['wA'], dsem='wA', after=[wtok['w_in0']])
                    uq = WB['w_uq'][l].rearrange("(c p) (h d) -> p c h d", p=128, d=96)
                    op('sp', lambda e: [e.dma_start(out=wqn[:, c, :].rearrange("p (h d) -> p h d", d=64), in_=uq[:, c, :, 0:64]) for c in range(3)]
                       + [e.dma_start(out=wqr1[:, c, :].rearrange("p (h d) -> p h d", d=16), in_=uq[:, c, :, 64:80]) for c in range(3)]
                       + [e.dma_start(out=wqr2[:, c, :].rearrange("p (h d) -> p h d", d=16), in_=uq[:, c, :, 80:96]) for c in range(3)],
                       w=['wq'], dsem='wA', after=[wtok['w_uq']])
                    ukv = WB['w_ukv'][l].rearrange("(c p) (h d) -> p c h d", p=128, d=128)
                    op('sp', lambda e: [e.dma_start(out=wkn[:, c, :].rearrange("p (h d) -> p h d", d=64), in_=ukv[:, c, :, 0:64]) for c in range(2)]
                       + [e.dma_start(out=wv[:, c, :].rearrange("p (h d) -> p h d", d=64), in_=ukv[:, c, :, 64:128]) for c in range(2)],
                       w=['wkv'], dsem='wA', after=[wtok['w_ukv']])
                gq = load_cols(st, "gqc", W['g_q'][l].rearrange("(c p) -> c p", p=128), 3, 'gq')
                gkv = load_cols(st, "gkvc", W['g_kv'][l].rearrange("(c p) -> c p", p=128), 2, 'gkv')
                finish_cols()

                xt = [sb(f"xt{i}", [128, 4, D], F32) for i in range(2)]
                rC = [sb(f"rC{i}", [128, T], F32) for i in range(2)]
                rS = [sb(f"rS{i}", [128, T], F32) for i in range(2)]
                xb = sb("xb", [128, 4, D], BF16)
                xT = [sb(f"xTa{i}", [128, KC, T], BF16) for i in range(2)]
                cqg = sb("cqg", [128, 3, T], BF16)
                sq = sb("sq", [128, 3, T], BF16)
                ckvg = sb("ckvg", [128, 2, T], BF16)
                sqkv = sb("sqkv", [128, 2, T], BF16)
                rq = sb("rq", [128, T], F32)
                rkv = sb("rkv", [128, T], F32)
                lnt = sb("lnt", [128, T], F32)
                lnt4 = sb("lnt4", [128, 4], F32)
                rtok = sb("rtok", [128, 4], F32)
                x1f = sb("x1f", [128, T], F32)
                x2f = sb("x2f", [128, T], F32)
                Cr = sb("Cr", [128, T], F32)
                Sr = sb("Sr", [128, T], F32)
                ta = sb("ta", [128, T], F32)
                tb = sb("tb", [128, T], F32)
                qn_st = [sb(f"qn_st{i}", [128, 4, T], BF16) for i in range(2)]
                qr1_st = [sb(f"qr1_st{i}", [128, T], BF16) for i in range(2)]
                qr2_st = [sb(f"qr2_st{i}", [128, T], BF16) for i in range(2)]
                kn_st = [sb(f"kn_st{i}", [128, 4, T], BF16) for i in range(2)]
                kr1_st = [sb(f"kr1_st{i}", [16, T], BF16) for i in range(2)]
                kr2_st = [sb(f"kr2_st{i}", [16, T], BF16) for i in range(2)]

                def load(i):
                    sl = i % 2
                    t0 = i * T
                    op('sp', lambda e: e.dma_start(out=xt[sl][:], in_=x_src[t0:t0 + T, :].rearrange("(j p) d -> p j d", p=128)),
                       w=[('xt', sl)], dsem=f'xt{sl}')
                    op('sp', lambda e: [e.dma_start(out=rC[sl][:], in_=ropeC[:, t0:t0 + T]),
                                        e.dma_start(out=rS[sl][:], in_=ropeS[:, t0:t0 + T])],
                       w=[('rope', sl)], dsem=f'rope{sl}')

                def cast(i_):
                    for j in range(4):
                        if j % 2 == 0:
                            op('pool', lambda e, j=j: e.tensor_copy(out=xb[:, j, :], in_=xt[i_ % 2][:, j, :]),
                               r=[('xt', i_ % 2)], w=[('xb', j)])
                        else:
                            op('act', lambda e, j=j: e.activation(out=xb[:, j, :], in_=xt[i_ % 2][:, j, :], func=AF.Copy),
                               r=[('xt', i_ % 2)], w=[('xb', j)])

                def prep(i_):
                    sl_ = i_ % 2
                    t0_ = i_ * T
                    for m in range(4):
                        b = P.psum()

                        def tr(e):
                            ins = None
                            for cc in range(2):
                                c = 2 * m + cc
                                for j in range(4):
                                    ins = e.transpose(PSB16(b)[:, cc * 512 + j * 128: cc * 512 + (j + 1) * 128],
                                                      xb[:, j, c * 128:(c + 1) * 128], identB[:])
                            return ins
                        op('pe', tr, r=[('xb', j) for j in range(4)] + ['identB'], w=[('ps', b)])
                        if m % 2 == 0:
                            op('act', lambda e: e.activation(out=xT[sl_][:, 2 * m:2 * m + 2, :].rearrange("p a t -> p (a t)"), in_=PSB16(b), func=AF.Copy),
                               r=[('ps', b)], w=[('xT', sl_, m)])
                        else:
                            op('dve', lambda e: e.tensor_copy(out=xT[sl_][:, 2 * m:2 * m + 2, :].rearrange("p a t -> p (a t)"), in_=PSB16(b)),
                               r=[('ps', b)], w=[('xT', sl_, m)])
                    op('sp', lambda e: e.dma_start(out=d['xT'][:, :, t0_:t0_ + T].rearrange("c p t -> p c t"), in_=xT[sl_][:]),
                       r=[('xT', sl_, m) for m in range(4)], dsem=f'xTst{sl_}')

                load(0)
                cast(0)
                prep(0)
                for i in range(NT):
                    sl = i % 2
                    t0 = i * T
                    if i + 1 < NT:
                        load(i + 1)
                    xTk = [('xT', sl, m) for m in range(4)]
                    for m in range(3):
                        b = P.psum()
                        mm_group(PS(b), [(wA[:, k, m * 128:(m + 1) * 128], xT[sl][:, k, :]) for k in range(KC)],
                                 r=xTk + ['wA'], w=[('ps', b)])
                        op('act', lambda e, m=m, b=b: e.activation(out=cqg[:, m, :], in_=PS(b), func=AF.Identity, scale=gq[:, m:m + 1]),
                           r=[('ps', b), 'gqc'], w=[('cqg', m)])
                        op('act', lambda e, m=m, b=b: e.activation(out=sq[:, m, :], in_=PS(b), func=AF.Square),
                           r=[('ps', b)], w=[('sq', m)])
                    b = P.psum()
                    mm_group(PS(b), [(onesB[:], sq[:, m, :]) for m in range(3)], r=[('sq', m) for m in range(3)] + ['onesB'],
                             w=[('ps', b)])
                    rstd_ops(rq[:], PS(b), lnt[:], 1.0 / QL, RMS_EPS, math.log(SM_SCALE), [('ps', b)], 'rq', 'lnt')
                    for j in range(4):
                        b = P.psum()
                        mm_group(PS(b), [(wqn[:, k, j * 128:(j + 1) * 128], cqg[:, k, :]) for k in range(3)],
                                 r=[('cqg', m) for m in range(3)] + ['wq'], w=[('ps', b)])
                        op('dve', lambda e, j=j, b=b: e.tensor_tensor(out=qn_st[sl][:, j, :], in0=PS(b), in1=rq[:], op=ALU.mult),
                           r=[('ps', b), 'rq'], w=[('qn_st', sl)])
                    if i + 1 < NT:
                        cast(i + 1)
                    b1 = P.psum()
                    mm_group(PS(b1), [(wqr1[:, k, :], cqg[:, k, :]) for k in range(3)], r=[('cqg', m) for m in range(3)] + ['wq'],
                             w=[('ps', b1)])
                    b2 = P.psum()
                    mm_group(PS(b2), [(wqr2[:, k, :], cqg[:, k, :]) for k in range(3)], r=[('cqg', m) for m in range(3)] + ['wq'],
                             w=[('ps', b2)])
                    op('act', lambda e: e.activation(out=x1f[:], in_=PS(b1), func=AF.Copy), r=[('ps', b1)], w=['x1f'])
                    op('act', lambda e: e.activation(out=x2f[:], in_=PS(b2), func=AF.Copy), r=[('ps', b2)], w=['x2f'])
                    op('pool', lambda e: e.tensor_tensor(out=Cr[:], in0=rC[sl][:], in1=rq[:], op=ALU.mult), r=[('rope', sl), 'rq'], w=['Cr'])
                    op('pool', lambda e: e.tensor_tensor(out=Sr[:], in0=rS[sl][:], in1=rq[:], op=ALU.mult), r=[('rope', sl), 'rq'], w=['Sr'])
                    op('dve', lambda e: e.tensor_tensor(out=ta[:], in0=x1f[:], in1=Cr[:], op=ALU.mult), r=['x1f', 'Cr'], w=['ta'])
                    op('pool', lambda e: e.tensor_tensor(out=tb[:], in0=x2f[:], in1=Sr[:], op=ALU.mult), r=['x2f', 'Sr'], w=['tb'])
                    op('dve', lambda e: e.tensor_tensor(out=qr1_st[sl][:], in0=ta[:], in1=tb[:], op=ALU.subtract), r=['ta', 'tb'], w=[('qr1_st', sl)])
                    op('dve', lambda e: e.tensor_tensor(out=ta[:], in0=x1f[:], in1=Sr[:], op=ALU.mult), r=['x1f', 'Sr'], w=['ta'])
                    op('pool', lambda e: e.tensor_tensor(out=tb[:], in0=x2f[:], in1=Cr[:], op=ALU.mult), r=['x2f', 'Cr'], w=['tb'])
                    op('dve', lambda e: e.tensor_tensor(out=qr2_st[sl][:], in0=ta[:], in1=tb[:], op=ALU.add), r=['ta', 'tb'], w=[('qr2_st', sl)])
                    op('sp', lambda e: [e.dma_start(out=d['qTn'][:, :, t0:t0 + T].rearrange("j p t -> p j t"), in_=qn_st[sl][:]),
                                        e.dma_start(out=d['qTr1'][:, t0:t0 + T], in_=qr1_st[sl][:]),
                                        e.dma_start(out=d['qTr2'][:, t0:t0 + T], in_=qr2_st[sl][:])],
                       r=[('qn_st', sl), ('qr1_st', sl), ('qr2_st', sl)], dsem=f'qst{sl}')
                    for m in range(2):
                        b = P.psum()
                        mm_group(PS(b), [(wA[:, k, QL + m * 128:QL + (m + 1) * 128], xT[sl][:, k, :]) for k in range(KC)],
                                 r=xTk + ['wA'], w=[('ps', b)])
                        op('act', lambda e, m=m, b=b: e.activation(out=ckvg[:, m, :], in_=PS(b), func=AF.Identity, scale=gkv[:, m:m + 1]),
                           r=[('ps', b), 'gkvc'], w=[('ckvg', m)])
                        op('act', lambda e, m=m, b=b: e.activation(out=sqkv[:, m, :], in_=PS(b), func=AF.Square),
                           r=[('ps', b)], w=[('sqkv', m)])
                    b = P.psum()
                    mm_group(PS(b), [(onesB[:], sqkv[:, m, :]) for m in range(2)], r=[('sqkv', m) for m in range(2)] + ['onesB'],
                             w=[('ps', b)])
                    rstd_ops(rkv[:], PS(b), lnt[:], 1.0 / KVL, RMS_EPS, 0.0, [('ps', b)], 'rkv', 'lnt')
                    b = P.psum()

                    def ssq_tok(e, b=b):
                        ins = None
                        for j in range(4):
                            for m in range(2):
                                ins = e.matmul(PS(b)[:, j:j + 1], sqkv[:, m, j * 128:(j + 1) * 128], onesB[:, 0:1],
                                               start=(m == 0), stop=(m == 1))
                        return ins
                    op('pe', ssq_tok, r=[('sqkv', m) for m in range(2)] + ['onesB'], w=[('ps', b)])
                    rstd_ops(rtok[:], PS(b)[:, 0:4], lnt4[:], 1.0 / KVL, RMS_EPS, 0.0, [('ps', b)], 'rtok', 'lnt4')
                    for j in range(4):
                        b = P.psum()
                        mm_group(PS(b), [(wkn[:, k, j * 128:(j + 1) * 128], ckvg[:, k, :]) for k in range(2)],
                                 r=[('ckvg', m) for m in range(2)] + ['wkv'], w=[('ps', b)])
                        op('dve', lambda e, j=j, b=b: e.tensor_tensor(out=kn_st[sl][:, j, :], in0=PS(b), in1=rkv[:], op=ALU.mult),
                           r=[('ps', b), 'rkv'], w=[('kn_st', sl)])
                    b1 = P.psum()
                    mm_group(PS(b1)[0:16, :], [(wA[:, k, 640:656], xT[sl][:, k, :]) for k in range(KC)], r=xTk + ['wA'], w=[('ps', b1)])
                    b2 = P.psum()
                    mm_group(PS(b2)[0:16, :], [(wA[:, k, 656:672], xT[sl][:, k, :]) for k in range(KC)], r=xTk + ['wA'], w=[('ps', b2)])
                    op('act', lambda e: e.activation(out=x1f[0:16, :], in_=PS(b1)[0:16, :], func=AF.Copy), r=[('ps', b1), ('qr1_st', sl), ('qr2_st', sl)], w=['x1f'])
                    op('act', lambda e: e.activation(out=x2f[0:16, :], in_=PS(b2)[0:16, :], func=AF.Copy), r=[('ps', b2), ('qr1_st', sl), ('qr2_st', sl)], w=['x2f'])
                    op('dve', lambda e: e.tensor_tensor(out=ta[0:16, :], in0=x1f[0:16, :], in1=rC[sl][0:16, :], op=ALU.mult), r=['x1f', ('rope', sl)], w=['ta'])
                    op('pool', lambda e: e.tensor_tensor(out=tb[0:16, :], in0=x2f[0:16, :], in1=rS[sl][0:16, :], op=ALU.mult), r=['x2f', ('rope', sl)], w=['tb'])
                    op('dve', lambda e: e.tensor_tensor(out=kr1_st[sl][:], in0=ta[0:16, :], in1=tb[0:16, :], op=ALU.subtract), r=['ta', 'tb'], w=[('kr1_st', sl)])
                    op('dve', lambda e: e.tensor_tensor(out=ta[0:16, :], in0=x1f[0:16, :], in1=rS[sl][0:16, :], op=ALU.mult), r=['x1f', ('rope', sl)], w=['ta'])
                    op('pool', lambda e: e.tensor_tensor(out=tb[0:16, :], in0=x2f[0:16, :], in1=rC[sl][0:16, :], op=ALU.mult), r=['x2f', ('rope', sl)], w=['tb'])
                    op('dve', lambda e: e.tensor_tensor(out=kr2_st[sl][:], in0=ta[0:16, :], in1=tb[0:16, :], op=ALU.add), r=['ta', 'tb'], w=[('kr2_st', sl)])
                    op('sp', lambda e: [e.dma_start(out=d['kTn'][:, :, t0:t0 + T].rearrange("j p t -> p j t"), in_=kn_st[sl][:]),
                                        e.dma_start(out=d['kTr1'][:, t0:t0 + T], in_=kr1_st[sl][:]),
                                        e.dma_start(out=d['kTr2'][:, t0:t0 + T], in_=kr2_st[sl][:])],
                       r=[('kn_st', sl), ('kr1_st', sl), ('kr2_st', sl)], dsem=f'kst{sl}')
                    if i + 1 < NT:
                        prep(i + 1)
                    for j in range(4):
                        b = P.psum()
                        mm_group(PS(b), [(ckvg[:, k, j * 128:(j + 1) * 128], wv[:, k, :]) for k in range(2)],
                                 r=[('ckvg', m) for m in range(2)] + ['wkv'], w=[('ps', b)])
                        blk = 4 * i + j
                        op('act', lambda e, j=j, b=b, blk=blk: e.activation(out=VresBox[0][:, blk, :, 0:64],
                                                                           in_=PS(b).rearrange("p (h d) -> p h d", d=64),
                                                                           func=AF.Identity, scale=rtok[:, j:j + 1]),
                           r=[('ps', b), 'rtok'], w=[('V', blk)])
            P.barrier()

        def phaseB(l, s):
            S = SL[s]
            NQ = S // T
            NB = S // 128
            d = sc[s]
            with ExitStack() as st:
                def sb(name, shape, dt):
                    return st.enter_context(SBT(name, shape, dt))
                kt = [sb(f"kt{i}", [96, S], BF16) for i in range(2)]
                qt = [sb(f"qt{i}", [96, T], BF16) for i in range(3)]
                pt = [sb(f"pt{i}", [128, 3 * T], BF16) for i in range(3)]
                ost = [sb(f"ost{i}", [65, T], F32) for i in range(2)]
                OB_ = [6, 7]
                GS = 3
                kgroups = [(k0, min(GS, NB - k0)) for k0 in range(0, NB, GS)]
                NG = len(kgroups)

                groups = [(h, qb, G) for h in range(NH) for qb in range(NQ) for G in range(NG)]
                N = len(groups)

                def load_k(h):
                    sl = h % 2
                    op('sp', lambda e: [e.dma_start(out=kt[sl][0:64, :], in_=d['kTn'][h // 2, (h % 2) * 64:(h % 2) * 64 + 64, :]),
                                        e.dma_start(out=kt[sl][64:80, :], in_=d['kTr1'][:, :]),
                                        e.dma_start(out=kt[sl][80:96, :], in_=d['kTr2'][:, :])],
                       w=[('kt', sl)], dsem=f'kt{sl}')

                def load_q(g):
                    h, qb = divmod(g, NQ)
                    sl = g % 3
                    t0 = qb * T
                    op('sp', lambda e: [e.dma_start(out=qt[sl][0:64, :], in_=d['qTn'][h // 2, (h % 2) * 64:(h % 2) * 64 + 64, t0:t0 + T]),
                                        e.dma_start(out=qt[sl][64:80, :], in_=d['qTr1'][16 * h:16 * h + 16, t0:t0 + T]),
                                        e.dma_start(out=qt[sl][80:96, :], in_=d['qTr2'][16 * h:16 * h + 16, t0:t0 + T])],
                       w=[('qt', sl)], dsem=f'qt{sl}')

                def SW(ws, gs_):
                    return psall[:, ws * 1536:ws * 1536 + gs_ * 512]

                load_k(0)
                load_q(0)
                if NH * NQ > 1:
                    load_q(1)
                for n in range(N + 1):
                    if n < N:
                        h, qb, G = groups[n]
                        k0, gs_ = kgroups[G]
                        g = h * NQ + qb
                        if G == 0:
                            if qb == 0 and h + 1 < NH:
                                load_k(h + 1)
                            if g + 2 < NH * NQ:
                                load_q(g + 2)
                        ws = n % 2
                        ps_ = n % 3

                        def qk(e):
                            ins = None
                            for u_ in range(gs_):
                                kb = k0 + u_
                                ins = e.matmul(SW(ws, gs_)[:, u_ * 512:(u_ + 1) * 512], kt[h % 2][0:96, kb * 128:(kb + 1) * 128],
                                               qt[g % 3][0:96, :], start=True, stop=True)
                            return ins
                        tk_ = op('pe', qk, r=[('kt', h % 2), ('qt', g % 3)], w=[('psw', ws)])
                        if cast_queue and n % 64 == 32:
                            cast_queue.pop(0)(tk_)
                        op('act', lambda e: e.activation(out=pt[ps_][:, 0:gs_ * 512], in_=SW(ws, gs_), func=AF.Exp),
                           r=[('psw', ws)], w=[('pt', ps_)])
                    if n >= 1:
                        m_ = n - 1
                        h, qb, G = groups[m_]
                        k0, gs_ = kgroups[G]
                        g = h * NQ + qb
                        bo = OB_[g % 2]
                        ps_ = m_ % 3

                        def pv(e):
                            ins = None
                            for u_ in range(gs_):
                                kb = k0 + u_
                                ins = e.matmul(PS(bo)[0:65, :], VresBox[0][:, kb, h, :], pt[ps_][:, u_ * 512:(u_ + 1) * 512],
                                               start=(kb == 0), stop=(kb == NB - 1))
                            return ins
                        op('pe', pv, r=[('pt', ps_), 'Vones'] + [('V', k0 + u_) for u_ in range(gs_)], w=[('ps', bo)])
                        if G == NG - 1:
                            t0 = qb * T
                            o_ = ost[g % 2]
                            op('dve', lambda e: e.tensor_copy(out=o_[0:64, :], in_=PS(bo)[0:64, :]),
                               r=[('ps', bo)], w=[('ost', g % 2)])
                            op('dve', lambda e: e.reciprocal(out=o_[64:65, :], in_=PS(bo)[64:65, :]),
                               r=[('ps', bo), ('ost', g % 2)], w=[('ost', g % 2)])
                            op('sp', lambda e: [e.dma_start(out=d['oaU'][h * 64:(h + 1) * 64, t0:t0 + T], in_=o_[0:64, :]),
                                                e.dma_start(out=d['sums'][h:h + 1, t0:t0 + T], in_=o_[64:65, :])],
                               r=[('ost', g % 2)], dsem=f'ost{g % 2}')
            P.barrier()

        def phaseC1(l):
            with ExitStack() as st:
                def sb(name, shape, dt):
                    return st.enter_context(SBT(name, shape, dt))
                wC = sb("wC", [128, KC, 1536], BF16)
                wsn = sb("wsn", [128, 4, 128], BF16)
                wsT = sb("wsT", [128, 4, 128], BF16)
                kmT2 = [sb(f"kmT{i}", [128, 4, 256], BF16) for i in range(2)]
                vm2 = [sb(f"vm{i}", [128, 2, 512], BF16) for i in range(2)]
                bsb4 = sb("bsb4", [128, 4, T], F32)
                op('sp', lambda e: e.dma_start(out=wC[:], in_=WB['w_in'][l, :, 672:2208].rearrange("(c p) n -> p c n", p=128)),
                   w=['wC'], dsem='wA', after=[wtok['w_in1']])
                op('sp', lambda e: e.dma_start(out=wsn[:], in_=WB['w_s'][l].rearrange("g p q -> p g q")), w=['wsn'], dsem='wA', after=[wtok['w_s']])
                for g in range(4):
                    op('sp', lambda e, g=g: [e.dma_start(out=bsb4[:, g, j * 128:(j + 1) * 128], in_=W['b_s'][l, g, :].partition_broadcast(128))
                                             for j in range(4)], w=[('bsb4', g)], dsem='c0')
                gmg = load_bcast(st, "gmg", W['gm_ln_g'][l], 512)
                gmb = load_bcast(st, "gmb", W['gm_ln_b'][l], 512)
                with ExitStack() as st2:
                    wm = st2.enter_context(SBT("wm", [128, KC, 1024], BF16))
                    mf = st2.enter_context(SBT("mf", [128, 2, D], F32))
                    mb = st2.enter_context(SBT("mb", [128, 2, D], BF16))
                    mT = st2.enter_context(SBT("mT", [128, KC, 256], BF16))
                    op('sp', lambda e: e.dma_start(out=wm[:], in_=WB['w_mem_kv'][l].rearrange("(c p) n -> p c n", p=128)), w=['wm'], dsem='wA', after=[wtok['w_mem_kv']])
                    P.barrier()
                    b = P.psum()

                    def trs(e, b=b):
                        ins = None
                        for g in range(4):
                            ins = e.transpose(PSB16(b)[:, g * 128:(g + 1) * 128], wsn[:, g, :], identB[:])
                        return ins
                    op('pe', trs, r=['wsn', 'identB'], w=[('ps', b)])
                    op('dve', lambda e: e.tensor_copy(out=wsT[:].rearrange("p g q -> p (g q)"), in_=PSB16(b)[:, 0:512]), r=[('ps', b)], w=['wsT'])
                    for s in range(2):
                        op('sp', lambda e: e.dma_start(out=mf[:], in_=mem_in[s].rearrange("(j p) d -> p j d", p=128)), w=['mf'], dsem='memf')
                        op('dve', lambda e: e.tensor_copy(out=mb[:], in_=mf[:]), r=['mf'], w=['mb'])
                        for m in range(2):
                            b = P.psum()

                            def trm(e, m=m, b=b):
                                ins = None
                                for cc in range(4):
                                    c = 4 * m + cc
                                    for j in range(2):
                                        ins = e.transpose(PSB16(b)[:, cc * 256 + j * 128:cc * 256 + (j + 1) * 128],
                                                          mb[:, j, c * 128:(c + 1) * 128], identB[:])
                                return ins
                            op('pe', trm, r=['mb', 'identB'], w=[('ps', b)])
                            op('dve', lambda e, m=m, b=b: e.tensor_copy(out=mT[:, 4 * m:4 * m + 4, :].rearrange("p a t -> p (a t)"), in_=PSB16(b)),
                               r=[('ps', b)], w=[('mT', m)])
                        for h in range(4):
                            b = P.psum()
                            mm_group(PS(b)[:, 0:256], [(wm[:, k, h * 128:(h + 1) * 128], mT[:, k, :]) for k in range(KC)],
                                     r=['wm', ('mT', 0), ('mT', 1)], w=[('ps', b)])
                            op('dve', lambda e, h=h, b=b: e.tensor_copy(out=kmT2[s][:, h, :], in_=PS(b)[:, 0:256]), r=[('ps', b)], w=['kmT'])
                        for j in range(2):
                            b = P.psum()
                            mm_group(PS(b), [(mT[:, k, j * 128:(j + 1) * 128], wm[:, k, 512:1024]) for k in range(KC)],
                                     r=['wm', ('mT', 0), ('mT', 1)], w=[('ps', b)])
                            op('dve', lambda e, j=j, b=b: e.tensor_copy(out=vm2[s][:, j, :], in_=PS(b)), r=[('ps', b)], w=['vm'])
                    P.barrier()

                xT = [sb(f"xTc{i}", [128, KC, T], BF16) for i in range(2)]
                oaU = [sb(f"oaU{i}", [128, 4, T], F32) for i in range(2)]
                smb = [sb(f"smb{i}", [128, 4, T], F32) for i in range(2)]
                u2 = [sb(f"u{i}", [128, 4, T], F32) for i in range(2)]
                gv = sb("gv", [128, 4, T], F32)
                vb = sb("vb", [128, 4, T], BF16)
                stats = sb("stats", [128, 4, 1, 6], F32)
                mv = sb("mv", [128, 4, 2], F32)
                tmp4 = sb("tmp4", [128, 4], F32)
                rs4 = sb("rs4", [128, 4], F32)
                nmr4 = sb("nmr4", [128, 4], F32)
                qm = sb("qm", [128, 4, T], BF16)
                pm = [sb(f"pm{i}", [128, T], BF16) for i in range(2)]
                rsum = sb("rsum", [128, T], F32)
                tsp = sb("tsp", [128, T], F32)
                brst = [sb(f"brst{i}", [128, 12, T], BF16) for i in range(2)]

                tiles = [(s_, i_) for s_ in range(2) for i_ in range(SL[s_] // T)]

                def load(n_):
                    s_, i_ = tiles[n_]
                    d = sc[s_]
                    sl = n_ % 2
                    t0 = i_ * T
                    op('sp', lambda e: e.dma_start(out=xT[sl][:], in_=d['xT'][:, :, t0:t0 + T].rearrange("c p t -> p c t")),
                       w=[('xT', sl)], dsem=f'xT{sl}')
                    op('sp', lambda e: e.dma_start(out=oaU[sl][:], in_=d['oaU'][:, t0:t0 + T].rearrange("(c p) t -> p c t", p=128)),
                       w=[('oaU', sl)], dsem=f'oaU{sl}')
                    op('sp', lambda e: [e.dma_start(out=smb[sl][(hh % 2) * 64:(hh % 2) * 64 + 64, hh // 2, :],
                                                    in_=d['sums'][hh, t0:t0 + T].partition_broadcast(64)) for hh in range(NH)],
                       w=[('smb', sl)], dsem=f'smb{sl}')

                pend_mix = []

                def mk_mix(sl, d, t0):
                    bst = brst[sl]
                    u = u2[sl]

                    def run():
                        for g in range(4):
                            b = P.psum()

                            def mix(e, g=g, b=b):
                                ins = None
                                for j in range(4):
                                    ins = e.matmul(PS(b)[:, j * 128:(j + 1) * 128], vb[:, j, g * 128:(g + 1) * 128], wsT[:, g, :], start=True, stop=True)
                                return ins
                            op('pe', mix, r=[('vb', j) for j in range(4)] + ['wsT'], w=[('ps', b)])
                            op('dve', lambda e, g=g, b=b: e.tensor_tensor(out=tsp[:], in0=PS(b), in1=bsb4[:, g, :], op=ALU.add),
                               r=[('ps', b), ('bsb4', g)], w=['tsp'])
                            op('dve', lambda e, g=g: e.tensor_tensor(out=bst[:, 4 + g, :], in0=tsp[:], in1=u[:, g, :], op=ALU.mult),
                               r=['tsp', ('u', sl, g)], w=[('brst', sl, 4 + g)])

                        op('sp', lambda e: e.dma_start(out=d['brT'][:, :, t0:t0 + T].rearrange("c p t -> p c t"), in_=bst[:]),
                           r=[('brst', sl, c) for c in range(12)], dsem=f'brst{sl}')
                    return run

                load(0)
                for n_, (s, i) in enumerate(tiles):
                    d = sc[s]
                    sl = n_ % 2
                    t0 = i * T
                    if n_ + 1 < len(tiles):
                        load(n_ + 1)
                    xk = [('xT', sl)]
                    u = u2[sl]
                    bst = brst[sl]
                    for c in range(4):
                        b = P.psum()
                        mm_group(PS(b), [(wC[:, k, c * 128:(c + 1) * 128], xT[sl][:, k, :]) for k in range(KC)], r=xk + ['wC'], w=[('ps', b)])
                        op('act', lambda e, c=c, b=b: e.activation(out=u[:, c, :], in_=PS(b), func=AF.Gelu_apprx_tanh), r=[('ps', b)], w=[('u', sl, c)])
                    for j in range(4):
                        b = P.psum()
                        mm_group(PS(b), [(xT[sl][:, k, j * 128:(j + 1) * 128], wC[:, k, 512:1024]) for k in range(KC)], r=xk + ['wC'], w=[('ps', b)])
                        op('act', lambda e, j=j, b=b: e.activation(out=gv[:, j, :], in_=PS(b), func=AF.Gelu_apprx_tanh), r=[('ps', b)], w=[('gv', j)])
                    for h in range(4):
                        b = P.psum()
                        mm_group(PS(b), [(wC[:, k, 1024 + h * 128:1024 + (h + 1) * 128], xT[sl][:, k, :]) for k in range(KC)], r=xk + ['wC'], w=[('ps', b)])
                        op('act', lambda e, h=h, b=b: e.activation(out=qm[:, h, :], in_=PS(b), func=AF.Identity, scale=MEM_SCALE), r=[('ps', b)], w=[('qm', h)])
                    while pend_mix:
                        pend_mix.pop(0)()
                    ln_rows(gv, 4, 512, gmg, gmb, mv, tmp4, rs4, 'gv', LN_EPS, stats=stats, gb_eng='dve',
                            fin_out=lambda j: vb[:, j, :], fin_key='vb', nmr=nmr4, dve_sub=(0, 1, 2, 3))
                    for h in range(4):
                        bpv = P.psum()
                        bsm = P.psum()
                        for blk in range(2):
                            bs = P.psum()
                            mm_group(PS(bs), [(kmT2[s][:, h, blk * 128:(blk + 1) * 128], qm[:, h, :])], r=['kmT', ('qm', h)], w=[('ps', bs)])
                            pp = pm[blk]
                            op('act', lambda e, pp=pp, bs=bs: e.activation(out=pp[:], in_=PS(bs), func=AF.Exp), r=[('ps', bs)], w=[('pm', blk)])
                            mm_group(PS(bpv), [(vm2[s][:, blk, h * 128:(h + 1) * 128], pp[:])], r=['vm', ('pm', blk)], w=[('ps', bpv)],
                                     start=(blk == 0), stop=(blk == 1))
                            mm_group(PS(bsm), [(onesB[:], pp[:])], r=['onesB', ('pm', blk)], w=[('ps', bsm)],
                                     start=(blk == 0), stop=(blk == 1))
                        op('act', lambda e, bsm=bsm: e.activation(out=rsum[:], in_=PS(bsm), func=AF.Ln), r=[('ps', bsm)], w=['rsum'])
                        op('act', lambda e: e.activation(out=rsum[:], in_=rsum[:], func=AF.Exp, scale=-1.0), r=['rsum'], w=['rsum'])
                        op('dve', lambda e, h=h, bpv=bpv: e.tensor_tensor(out=bst[:, 8 + h, :], in0=PS(bpv), in1=rsum[:], op=ALU.mult),
                           r=[('ps', bpv), 'rsum'], w=[('brst', sl, 8 + h)])
                    for c in range(4):
                        op('pool', lambda e, c=c: e.tensor_tensor(out=bst[:, c, :], in0=oaU[sl][:, c, :], in1=smb[sl][:, c, :], op=ALU.mult),
                           r=[('oaU', sl), ('smb', sl)], w=[('brst', sl, c)])
                    pend_mix.append(mk_mix(sl, d, t0))
                while pend_mix:
                    pend_mix.pop(0)()
            P.barrier()

        def phaseC2(l, x_srcs):
            with ExitStack() as st:
                def sb(name, shape, dt):
                    return st.enter_context(SBT(name, shape, dt))
                wG = sb("wG", [128, KC, 3072], BF16)
                wbr = sb("wbr", [128, 12, D], BF16)
                wo = sb("wo", [128, KC, D], BF16)
                op('sp', lambda e: e.dma_start(out=wG[:], in_=WB['w_in'][l, :, 2208:5280].rearrange("(c p) n -> p c n", p=128)), w=['wG'], dsem='wA', after=[wtok['w_in2']])
                for bi, nm in enumerate(['w_br_mla', 'w_br_gmlp', 'w_br_mem']):
                    op('sp', lambda e, bi=bi, nm=nm: e.dma_start(out=wbr[:, 4 * bi:4 * bi + 4, :], in_=WB[nm][l].rearrange("(c p) n -> p c n", p=128)),
                       w=[('wbr', bi)], dsem='wA', after=[wtok['w_br_mla'], wtok['w_br_gmlp'], wtok['w_br_mem']])
                op('sp', lambda e: e.dma_start(out=wo[:], in_=WB['w_o'][l].rearrange("(c p) n -> p c n", p=128)), w=['wo'], dsem='wA', after=[wtok['w_o']])
                bg = load_cols(st, "bgc", W['b_gate'][l].rearrange("(c p) -> c p", p=128), 24, 'bg')
                g1 = load_bcast(st, "g1", W['ln1_g'][l], D)
                b1 = load_bcast(st, "b1", W['ln1_b'][l], D)
                finish_cols()

                xT = [sb(f"xTd{i}", [128, KC, T], BF16) for i in range(2)]
                br = [sb(f"brd{i}", [128, 12, T], BF16) for i in range(2)]
                gt = [sb(f"gt{i}", [128, T], BF16) for i in range(6)]
                t1 = [sb(f"t1_{i}", [128, T], F32) for i in range(3)]
                mg = sb("mg", [128, KC, T], BF16)
                r4 = [sb("r4_0", [128, 4, D], F32)] * 2
                yb = sb("yb", [128, 4, D], BF16)
                yTst = [sb("yTst0", [128, KC, T], BF16)] * 2
                stats = sb("stats2", [128, 4, 2, 6], F32)
                mv = sb("mv2", [128, 4, 2], F32)
                tmp4 = sb("tmp42", [128, 4], F32)
                rs4 = sb("rs42", [128, 4], F32)
                nmr4 = sb("nmr42", [128, 4], F32)

                tiles = [(s_, i_) for s_ in range(2) for i_ in range(SL[s_] // T)]

                def load(n_):
                    s_, i_ = tiles[n_]
                    d = sc[s_]
                    sl = n_ % 2
                    t0 = i_ * T
                    op('sp', lambda e: e.dma_start(out=xT[sl][:], in_=d['xT'][:, :, t0:t0 + T].rearrange("c p t -> p c t")), w=[('xT', sl)], dsem=f'xT{sl}')
                    op('sp', lambda e: e.dma_start(out=br[sl][:], in_=d['brT'][:, :, t0:t0 + T].rearrange("c p t -> p c t")), w=[('br', sl)], dsem=f'br{sl}')

                pend_tr = []

                def emit_tr():
                    while pend_tr:
                        d_, t0_ = pend_tr.pop(0)
                        for m in range(4):
                            b = P.psum()

                            def tr(e, m=m, b=b):
                                ins = None
                                for cc in range(2):
                                    c = 2 * m + cc
                                    for j in range(4):
                                        ins = e.transpose(PSB16(b)[:, cc * 512 + j * 128: cc * 512 + (j + 1) * 128],
                                                          yb[:, j, c * 128:(c + 1) * 128], identB[:])
                                return ins
                            op('pe', tr, r=[('yb', j) for j in range(4)] + ['identB'], w=[('ps', b)])
                            op('dve', lambda e, m=m, b=b: e.tensor_copy(out=yTst[0][:, 2 * m:2 * m + 2, :].rearrange("p a t -> p (a t)"), in_=PSB16(b)),
                               r=[('ps', b)], w=[('yTst', 0, m)])
                        op('sp', lambda e, t0_=t0_, d_=d_: e.dma_start(out=d_['yT'][:, :, t0_:t0_ + T].rearrange("c p t -> p c t"), in_=yTst[0][:]),
                           r=[('yTst', 0, m) for m in range(4)], dsem='yTst0')

                load(0)
                gi = 0
                for n_, (s, i) in enumerate(tiles):
                    d = sc[s]
                    x_src = x_srcs[s]
                    sl = n_ % 2
                    t0 = i * T
                    if n_ + 1 < len(tiles):
                        load(n_ + 1)
                    R = r4[sl]
                    op('sp', lambda e: e.dma_start(out=R[:], in_=x_src[t0:t0 + T, :].rearrange("(j p) d -> p j d", p=128)),
                       w=[(('r4', 0), j) for j in range(4)], dsem='r4ld')
                    for m in range(8):
                        gts = []
                        for bi in range(3):
                            b = P.psum()
                            col = (bi * 8 + m) * 128
                            mm_group(PS(b), [(wG[:, k, col:col + 128], xT[sl][:, k, :]) for k in range(KC)], r=[('xT', sl), 'wG'], w=[('ps', b)])
                            gsl = gi % 6
                            gi += 1
                            op('act', lambda e, b=b, gsl=gsl, bi=bi, m=m: e.activation(out=gt[gsl][:], in_=PS(b), func=AF.Sigmoid,
                                                                                    bias=bg[:, bi * 8 + m:bi * 8 + m + 1]),
                               r=[('ps', b), 'bgc'], w=[('gt', gsl)])
                            gts.append(gsl)
                        for bi in range(3):
                            b = P.psum()
                            mm_group(PS(b), [(wbr[:, 4 * bi + k, m * 128:(m + 1) * 128], br[sl][:, 4 * bi + k, :]) for k in range(4)],
                                     r=[('br', sl), ('wbr', bi)], w=[('ps', b)])
                            op('dve', lambda e, b=b, bi=bi, gsl=gts[bi]: e.tensor_tensor(out=t1[bi][:], in0=PS(b), in1=gt[gsl][:], op=ALU.mult),
                               r=[('ps', b), ('gt', gts[bi])], w=[('t1', bi)])
                        op('dve', lambda e: e.tensor_tensor(out=t1[0][:], in0=t1[0][:], in1=t1[1][:], op=ALU.add), r=[('t1', 0), ('t1', 1)], w=[('t1', 0)])
                        op('dve', lambda e, m=m: e.tensor_tensor(out=mg[:, m, :], in0=t1[0][:], in1=t1[2][:], op=ALU.add),
                           r=[('t1', 0), ('t1', 2)], w=[('mg', m)])
                    emit_tr()
                    for j in range(4):
                        for hf in range(2):
                            b = P.psum()
                            mm_group(PS(b), [(mg[:, k, j * 128:(j + 1) * 128], wo[:, k, hf * 512:(hf + 1) * 512]) for k in range(KC)],
                                     r=[('mg', k) for k in range(KC)] + ['wo'], w=[('ps', b)])
                            op('dve', lambda e, j=j, hf=hf, b=b: e.scalar_tensor_tensor(out=R[:, j, hf * 512:(hf + 1) * 512],
                                                                                       in0=R[:, j, hf * 512:(hf + 1) * 512], scalar=ALPHA,
                                                                                       in1=PS(b), op0=ALU.mult, op1=ALU.add),
                               r=[('ps', b), (('r4', 0), j)], w=[(('r4', 0), j)])
                    ln_rows(R, 4, D, g1, b1, mv, tmp4, rs4, ('r4', 0), LN_EPS, stats=stats, nmr=nmr4)
                    op('sp', lambda e, t0=t0: e.dma_start(out=d['y'][t0:t0 + T, :].rearrange("(j p) d -> p j d", p=128), in_=R[:]),
                       r=[(('r4', 0), j) for j in range(4)], dsem=f'yst{sl}')
                    for j in range(4):
                        op('pool', lambda e, j=j: e.tensor_copy(out=yb[:, j, :], in_=R[:, j, :]), r=[(('r4', 0), j)], w=[('yb', j)])
                    pend_tr.append((d, t0))
                emit_tr()
            P.barrier()

        def phaseD(l, hh, dsts):
            NP = NPAIR // 2
            with ExitStack() as st:
                def sb(name, shape, dt):
                    return st.enter_context(SBT(name, shape, dt))
                wfa = sb("wfa", [128, KC, NP * 128], BF16)
                wfg = sb("wfg", [128, KC, NP * 128], BF16)
                wfo = sb("wfo", [128, NP, D], BF16)
                a0 = hh * NP * 128
                op('sp', lambda e: e.dma_start(out=wfa[:], in_=WB['w_ffn_in'][l, :, a0:a0 + NP * 128].rearrange("(c p) n -> p c n", p=128)), w=['wfa'], dsem='wA', after=[wtok['w_ffn_in']])
                op('sp', lambda e: e.dma_start(out=wfg[:], in_=WB['w_ffn_in'][l, :, DFF + a0:DFF + a0 + NP * 128].rearrange("(c p) n -> p c n", p=128)), w=['wfg'], dsem='wA', after=[wtok['w_ffn_in']])
                op('sp', lambda e: e.dma_start(out=wfo[:], in_=WB['w_ffn_out'][l, a0:a0 + NP * 128, :].rearrange("(c p) n -> p c n", p=128)), w=['wfo'], dsem='wA', after=[wtok['w_ffn_out']])
                cwa = [load_cols(st, f"cwa{t_}", W['conv_w'][l, t_, a0:a0 + NP * 128].rearrange("(c p) -> c p", p=128), NP, 'cw') for t_ in range(3)]
                cwg = [load_cols(st, f"cwg{t_}", W['conv_w'][l, t_, DFF + a0:DFF + a0 + NP * 128].rearrange("(c p) -> c p", p=128), NP, 'cw') for t_ in range(3)]
                cba = load_cols(st, "cba", W['conv_b'][l, a0:a0 + NP * 128].rearrange("(c p) -> c p", p=128), NP, 'cb')
                cbg = load_cols(st, "cbg", W['conv_b'][l, DFF + a0:DFF + a0 + NP * 128].rearrange("(c p) -> c p", p=128), NP, 'cb')
                g2 = b2 = None
                if hh == 1:
                    g2 = load_bcast(st, "g2", W['ln2_g'][l], D)
                    b2 = load_bcast(st, "b2", W['ln2_b'][l], D)
                finish_cols()

                yT = [sb(f"yTe{i}", [128, KC, T], BF16) for i in range(2)]
                E = [sb(f"E{i}", [128, T + 2], F32) for i in range(4)]
                acc = [sb(f"acc{i}", [128, T], F32) for i in range(4)]
                sg = [sb(f"sg{i}", [128, T], F32) for i in range(2)]
                sv = sb("sv", [128, 2 * NP, 2], F32)
                actT = [sb(f"actT{i}", [128, NP, T], BF16) for i in range(2)]
                R4 = [sb(f"R4_{i}", [128, 4, D], F32) for i in range(2)]
                Pt = [sb(f"Pt{i}", [128, 4, D], F32) for i in range(2)]
                stats = sb("stats3", [128, 4, 2, 6], F32)
                mv = sb("mv3", [128, 4, 2], F32)
                tmp4 = sb("tmp43", [128, 4], F32)
                rs4 = sb("rs43", [128, 4], F32)
                nmr4 = sb("nmr43", [128, 4], F32)
                for i_ in range(4):
                    op('dve', lambda e, i_=i_: e.memset(E[i_][:], 0.0), w=[('E', i_)])

                tiles = []
                for s_ in range(2):
                    nt_ = SL[s_] // T
                    tiles += [(s_, i_, False) for i_ in range(nt_)] + [(s_, nt_, True)]

                def load(n_):
                    s_, i_, fl_ = tiles[n_]
                    if fl_:
                        return
                    sl_ = n_ % 2
                    t0_ = i_ * T
                    op('sp', lambda e: e.dma_start(out=yT[sl_][:], in_=sc[s_]['yT'][:, :, t0_:t0_ + T].rearrange("c p t -> p c t")),
                       w=[('yT', sl_)], dsem=f'yT{sl_}')

                def win_rows(i, j):
                    w0 = i * T - 1 + 128 * j
                    if w0 < 0:
                        return 1, 0, 127
                    return 0, w0, 128

                def make_out(n, s, i, flush, sl):
                    S = SL[s]
                    d = sc[s]
                    dst = dsts[s]
                    nsub = 1 if flush else 4
                    A_ = actT[sl]
                    out = []
                    for j in range(nsub):
                        nrow = 1 if flush else 128
                        c0 = 0 if flush else j * 128
                        for hf in range(2):
                            def grp(j=j, hf=hf, nrow=nrow, c0=c0):
                                b = P.psum()
                                mm_group(PS(b)[0:nrow, :], [(A_[:, k, c0:c0 + nrow], wfo[:, k, hf * 512:(hf + 1) * 512]) for k in range(NP)],
                                         r=[('actT', sl, k) for k in range(NP)], w=[('ps', b)])
                                if hh == 0:
                                    op('act', lambda e: e.activation(out=Pt[sl][0:nrow, j, hf * 512:(hf + 1) * 512], in_=PS(b)[0:nrow, :], func=AF.Copy),
                                       r=[('ps', b)], w=[('Pt', sl)])
                                    return None

                                def evac():
                                    op('dve', lambda e: e.tensor_tensor(out=Pt[sl][0:nrow, j, hf * 512:(hf + 1) * 512],
                                                                        in0=PS(b)[0:nrow, :], in1=Pt[sl][0:nrow, j, hf * 512:(hf + 1) * 512], op=ALU.add),
                                       r=[('ps', b), ('Pt', sl)], w=[('Pt', sl)])
                                    op('dve', lambda e: e.scalar_tensor_tensor(out=R4[sl][0:nrow, j, hf * 512:(hf + 1) * 512],
                                                                               in0=R4[sl][0:nrow, j, hf * 512:(hf + 1) * 512], scalar=ALPHA,
                                                                               in1=Pt[sl][0:nrow, j, hf * 512:(hf + 1) * 512], op0=ALU.mult, op1=ALU.add),
                                       r=[('Pt', sl), (('R4', sl), j)], w=[(('R4', sl), j)])
                                return evac
                            out.append(grp)

                    def fin():
                        if hh == 0:
                            def stp(e):
                                if flush:
                                    return [e.dma_start(out=d['part'][S:S + 1, :], in_=Pt[sl][0:1, 0, :])]
                                t0 = i * T
                                return [e.dma_start(out=d['part'][t0:t0 + T, :].rearrange("(j p) d -> p j d", p=128), in_=Pt[sl][:])]
                            op('sp', stp, r=[('Pt', sl)], dsem=f'Ptst{sl}')
                        else:
                            ln_rows(R4[sl], nsub, D, g2, b2, mv, tmp4, rs4, ('R4', sl), LN_EPS, stats=stats,
                                    prow=(slice(0, 1) if flush else None), nmr=nmr4, dve_sub=(0, 1, 2, 3))

                            def sto(e):
                                if flush:
                                    return [e.dma_start(out=dst[S - 1:S, :], in_=R4[sl][0:1, 0, :])]
                                o_ = []
                                for j in range(4):
                                    p0, tk, n_ = win_rows(i, j)
                                    o_.append(e.dma_start(out=dst[tk:tk + n_, :], in_=R4[sl][p0:p0 + n_, j, :]))
                                return o_
                            op('sp', sto, r=[(('R4', sl), j) for j in range(4)], dsem=f'ost{sl}')
                    out.append(fin)
                    return out

                ei = 0
                pending = []
                late_ev = []

                def drain_pending():
                    while pending:
                        if len(pending) == 1:
                            while late_ev:
                                late_ev.pop(0)()
                        ev_ = pending.pop(0)()
                        if ev_ is not None:
                            late_ev.append(ev_)
                    while late_ev:
                        late_ev.pop(0)()

                load(0)
                for n, (s, i, flush) in enumerate(tiles):
                    S = SL[s]
                    d = sc[s]
                    sl = n % 2
                    WC = 1 if flush else T
                    nsub = 1 if flush else 4
                    if i == 0:
                        op('dve', lambda e: e.memset(sv[:], 0.0), w=['sv'])
                    if n + 1 < len(tiles):
                        load(n + 1)
                    if hh == 1:
                        def ldw(e):
                            o_ = []
                            for j in range(nsub):
                                if flush:
                                    o_.append(e.dma_start(out=R4[sl][0:1, 0, :], in_=d['y'][S - 1:S, :]))
                                    o_.append(e.dma_start(out=Pt[sl][0:1, 0, :], in_=d['part'][S:S + 1, :]))
                                else:
                                    p0, tk, n_ = win_rows(i, j)
                                    o_.append(e.dma_start(out=R4[sl][p0:p0 + n_, j, :], in_=d['y'][tk:tk + n_, :]))
                                    o_.append(e.dma_start(out=Pt[sl][p0:p0 + n_, j, :], in_=d['part'][tk + 1:tk + 1 + n_, :]))
                            return o_
                        op('sp', ldw, w=[(('R4', sl), j) for j in range(4)] + [('Pt', sl)], dsem=f'R4{sl}')
                    prev_s2 = None
                    for pj in range(NP):
                        accs = []
                        for ag in range(2):
                            wsrc = wfa if ag == 0 else wfg
                            ch = ag * NP + pj
                            cw = cwa if ag == 0 else cwg
                            cb = cba if ag == 0 else cbg
                            esl = ei % 4
                            ei += 1
                            Et = E[esl]
                            At = acc[esl]
                            op('pool', lambda e: e.tensor_copy(out=Et[:, 0:2], in_=sv[:, ch, :]), r=['sv'], w=[('E', esl)])
                            if not flush:
                                b = P.psum()
                                mm_group(PS(b), [(wsrc[:, k, pj * 128:(pj + 1) * 128], yT[sl][:, k, :]) for k in range(KC)],
                                         r=[('yT', sl)], w=[('ps', b)])
                                op('act', lambda e: e.activation(out=Et[:, 2:T + 2], in_=PS(b), func=AF.Copy), r=[('ps', b)], w=[('E', esl)])
                                op('pool', lambda e: e.tensor_copy(out=sv[:, ch, :], in_=Et[:, T:T + 2]), r=[('E', esl)], w=['sv'])
                            else:
                                op('pool', lambda e: e.memset(Et[:, 2:3], 0.0), w=[('E', esl)])
                            op('act', lambda e: e.activation(out=At[:, 0:WC], in_=Et[:, 1:1 + WC], func=AF.Identity,
                                                             bias=cb[:, pj:pj + 1], scale=cw[1][:, pj:pj + 1]),
                               r=[('E', esl)], w=[('acc', esl)])
                            op('dve', lambda e: e.scalar_tensor_tensor(out=At[:, 0:WC], in0=Et[:, 0:WC], scalar=cw[0][:, pj:pj + 1],
                                                                       in1=At[:, 0:WC], op0=ALU.mult, op1=ALU.add),
                               r=[('E', esl), ('acc', esl)], w=[('acc', esl)])
                            op('dve', lambda e: e.scalar_tensor_tensor(out=At[:, 0:WC], in0=Et[:, 2:2 + WC], scalar=cw[2][:, pj:pj + 1],
                                                                       in1=At[:, 0:WC], op0=ALU.mult, op1=ALU.add),
                               r=[('E', esl), ('acc', esl)], w=[('acc', esl)])
                            accs.append((At, esl))
                        def stage2(pj=pj, accs=accs):
                            ssl = pj % 2
                            op('act', lambda e: e.activation(out=sg[ssl][:, 0:WC], in_=accs[0][0][:, 0:WC], func=AF.Silu),
                               r=[('acc', accs[0][1])], w=[('sg', ssl)])
                            op('dve', lambda e: e.tensor_tensor(out=actT[sl][:, pj, 0:WC], in0=sg[ssl][:, 0:WC], in1=accs[1][0][:, 0:WC], op=ALU.mult),
                               r=[('sg', ssl), ('acc', accs[1][1])], w=[('actT', sl, pj)])
                        if prev_s2 is not None:
                            prev_s2()
                        prev_s2 = stage2
                        if late_ev:
                            late_ev.pop(0)()
                        if len(pending) > 1:
                            ev_ = pending.pop(0)()
                            if ev_ is not None:
                                late_ev.append(ev_)
                    prev_s2()
                    prev_s2 = None
                    drain_pending()
                    pending = make_out(n, s, i, flush, sl)
                drain_pending()
            P.barrier()

        for l in range(L):
            x_srcs = [x_in[s_] if l == 0 else sc[s_]['x1'] for s_ in range(2)]
            dsts = [sc[s_]['x1'] if l == 0 else y_out[s_] for s_ in range(2)]
            for s in range(2):
                with SBT("Vres", [128, SL[s] // 128, NH, 65], BF16) as Vres_:
                    VresBox[0] = Vres_
                    op('dve', lambda e: e.memset(Vres_[:, :, :, 64:65], 1.0), w=['Vones'])
                    phaseA(l, s, x_srcs[s])
                    phaseB(l, s)
                    if l == 0 and s == 1:
                        cast_rest()
            phaseC1(l)
            phaseC2(l, x_srcs)
            phaseD(l, 0, dsts)
            phaseD(l, 1, dsts)
        P.barrier(final=True)
    return nc


def _consts():
    identf = np.eye(128, dtype=np.float32)
    pos = np.arange(8192, dtype=np.float32)
    inv = (np.float32(10000.0) ** (-np.arange(0, 32, 2, dtype=np.float32) / np.float32(32))).astype(np.float32)
    ang = (pos[:, None] * inv[None, :]).astype(np.float32)
    c = np.cos(ang).astype(np.float32).T
    s_ = np.sin(ang).astype(np.float32).T
    return identf, np.ascontiguousarray(np.tile(c, (8, 1))), np.ascontiguousarray(np.tile(s_, (8, 1)))


_NC_CACHE = {}


def run(inputs, SP, SS, debug=False, trace=False):
    key = (SP, SS, debug)
    if key not in _NC_CACHE:
        _NC_CACHE[key] = build_nc(SP, SS, debug)
    nc = _NC_CACHE[key]
    identf, rc, rs = _consts()
    wd = {n: np.ascontiguousarray(np.asarray(inputs[n], dtype=np.float32)) for n in WNAMES}
    in_maps = []
    for b in range(8):
        m = dict(wd)
        m['xp'] = np.ascontiguousarray(inputs['x_prompt'][b])
        m['xs'] = np.ascontiguousarray(inputs['x_sample'][b])
        m['memp'] = np.ascontiguousarray(inputs['mem_prompt'][b])
        m['mems'] = np.ascontiguousarray(inputs['mem_sample'][b])
        m['identf'] = identf
        m['ropeC'] = rc
        m['ropeS'] = rs
        in_maps.append(m)
    res = run_bass_kernel_spmd(nc, in_maps, core_ids=list(range(8)), trace=trace)
    return res


def kernel(**inputs):
    inputs = {k: np.asarray(v) for k, v in inputs.items()}
    SP = inputs['x_prompt'].shape[1]
    SS = inputs['x_sample'].shape[1]
    res = run(inputs, SP, SS)
    yp = np.stack([res.results[b]['yp'] for b in range(8)], axis=0).astype(np.float32)
    ys = np.stack([res.results[b]['ys'] for b in range(8)], axis=0).astype(np.float32)
    return (yp, ys)
```
